# Optimizing a Trainium2 kernel written in Bass

```python
import math
import jax, jax.numpy as jnp
from jax import lax
import numpy as np

D_MODEL = 2048
BATCH = 4
SEQ = 8192
DEPTH = 1

N_META = 16
CONV_K = 4
GDN_HEADS = 8
GDN_DK = 128
GDN_DV = 128
GDN_CHUNK = 64
GLA_HEADS = 4
GLA_DK = 128
GLA_DV = 256
GLA_CHUNK = 16
GLA_GATE_RANK = 16
GLA_GATE_NORMALIZER = 16.0
GDN_QK = GDN_HEADS * GDN_DK
GDN_V = GDN_HEADS * GDN_DV
GLA_QK = GLA_HEADS * GLA_DK
GLA_V = GLA_HEADS * GLA_DV
MIX_WIDTH = GDN_V + GLA_V
D_FF = -(-8 * D_MODEL // (3 * 256)) * 256
IN_SPLITS = (2 * GDN_QK + GDN_V, GDN_V, GDN_HEADS, GDN_HEADS, GLA_QK, GLA_QK, GLA_V, GLA_V, GLA_GATE_RANK)
D_IN = sum(IN_SPLITS)
IN_OFFSETS = tuple(int(i) for i in np.cumsum(IN_SPLITS)[:-1])
NORM_EPS = 1e-6

kernel_name = "hybrid_gdn_gla_meta_block"


def rms_norm(x, w):
    xf = x.astype(jnp.float32)
    y = xf * lax.rsqrt(jnp.mean(xf * xf, axis=-1, keepdims=True) + NORM_EPS)
    return (y * w.astype(jnp.float32)).astype(x.dtype)


def l2_normalize(x):
    xf = x.astype(jnp.float32)
    return (xf * lax.rsqrt(jnp.sum(xf * xf, axis=-1, keepdims=True) + NORM_EPS)).astype(x.dtype)


def causal_short_conv(x, w):
    L = x.shape[1]
    xp = jnp.pad(x, ((0, 0), (CONV_K - 1, 0), (0, 0)))
    y = xp[:, 0:L] * w[0]
    for i in range(1, CONV_K):
        y = y + xp[:, i:i + L] * w[i]
    return jax.nn.silu(y)


def to_chunks(t, chunk, pad):
    t = jnp.pad(t, ((0, 0), (pad, 0), (0, 0), (0, 0)))
    b, lp, h, d = t.shape
    return t.reshape(b, lp // chunk, chunk, h, d).transpose(0, 3, 1, 2, 4)


def from_chunks(o, pad):
    b, h, n, c, d = o.shape
    return o.transpose(0, 2, 3, 1, 4).reshape(b, n * c, h, d)[:, pad:]


def gated_delta_rule(q, k, v, beta, g):
    out_dtype = v.dtype
    f32 = jnp.float32
    C = GDN_CHUNK
    pad = (-N_META) % C
    q, k, v = (to_chunks(t.astype(f32), C, pad) for t in (q, k, v))
    beta, g = (to_chunks(t.astype(f32)[..., None], C, pad)[..., 0] for t in (beta, g))
    gc = jnp.cumsum(g, axis=-1)
    causal = jnp.tril(jnp.ones((C, C), bool))
    strict = jnp.tril(jnp.ones((C, C), bool), -1)
    decay = jnp.exp(jnp.where(causal, gc[..., :, None] - gc[..., None, :], -jnp.inf))
    kb = k * beta[..., None]
    a_low = jnp.where(strict, jnp.einsum('bhncd,bhnsd->bhncs', kb, k) * decay, 0.0)
    t_mat = a_low + jnp.eye(C, dtype=f32)
    u = lax.linalg.triangular_solve(t_mat, v * beta[..., None], left_side=True, lower=True, unit_diagonal=True)
    w = lax.linalg.triangular_solve(t_mat, kb * jnp.exp(gc)[..., None], left_side=True, lower=True, unit_diagonal=True)
    qk = jnp.einsum('bhncd,bhnsd->bhncs', q, k) * decay
    q_dec = q * jnp.exp(gc)[..., None]
    k_dec = k * jnp.exp(gc[..., -1:] - gc)[..., None]
    g_last = jnp.exp(gc[..., -1])

    def step(S, inp):
        qd, kd, u_c, w_c, qk_c, gl = inp
        v_new = u_c - jnp.einsum('bhcd,bhde->bhce', w_c, S)
        o = jnp.einsum('bhcd,bhde->bhce', qd, S) + jnp.einsum('bhcs,bhse->bhce', qk_c, v_new)
        S = S * gl[..., None, None] + jnp.einsum('bhcd,bhce->bhde', kd, v_new)
        return S, o

    xs = tuple(jnp.moveaxis(t, 2, 0) for t in (q_dec, k_dec, u, w, qk, g_last))
    b, h = q.shape[0], q.shape[1]
    S0 = jnp.zeros((b, h, GDN_DK, GDN_DV), f32)
    _, o = lax.scan(step, S0, xs)
    return from_chunks(jnp.moveaxis(o, 0, 2), pad).astype(out_dtype)


def gla_chunked(q, k, v, log_a):
    out_dtype = v.dtype
    f32 = jnp.float32
    C = GLA_CHUNK
    pad = (-N_META) % C
    q, k, v, log_a = (to_chunks(t.astype(f32), C, pad) for t in (q, k, v, log_a))
    bcum = jnp.cumsum(log_a, axis=-2)
    causal = jnp.tril(jnp.ones((C, C), bool))

    def step(S, inp):
        q_c, k_c, v_c, b_c = inp
        diff = jnp.where(causal[..., None], b_c[..., :, None, :] - b_c[..., None, :, :], -jnp.inf)
        scores = jnp.einsum('bhid,bhjd,bhijd->bhij', q_c, k_c, jnp.exp(diff))
        o = jnp.einsum('bhid,bhde->bhie', q_c * jnp.exp(b_c), S) + jnp.einsum('bhij,bhje->bhie', scores, v_c)
        b_last = b_c[..., -1, :]
        S = S * jnp.exp(b_last)[..., None] + jnp.einsum(
            'bhjd,bhje->bhde', k_c * jnp.exp(b_last[..., None, :] - b_c), v_c)
        return S, o

    xs = tuple(jnp.moveaxis(t, 2, 0) for t in (q, k, v, bcum))
    b, h = q.shape[0], q.shape[1]
    S0 = jnp.zeros((b, h, GLA_DK, GLA_DV), f32)
    _, o = lax.scan(step, S0, xs)
    return from_chunks(jnp.moveaxis(o, 0, 2), pad).astype(out_dtype)


def setup_inputs(seed: int = 0) -> dict:
    key = jax.random.key(seed)
    ks = jax.random.split(key, 20)
    f32 = jnp.float32

    def nrm(k, shape, scale):
        return jax.random.normal(k, shape, f32) * scale

    def gain(k, shape):
        return 1.0 + 0.01 * jax.random.normal(k, shape, f32)

    dt = jnp.exp(jax.random.uniform(ks[7], (DEPTH, GDN_HEADS), f32, math.log(1e-3), math.log(1e-1)))
    return {
        "x": nrm(ks[0], (BATCH, SEQ, D_MODEL), 1.0),
        "meta_tokens": nrm(ks[1], (N_META, D_MODEL), 1.0),
        "attn_norm_w": gain(ks[2], (DEPTH, D_MODEL)),
        "w_in": nrm(ks[3], (DEPTH, D_MODEL, D_IN), D_MODEL ** -0.5),
        "gdn_conv_w": nrm(ks[4], (DEPTH, CONV_K, 2 * GDN_QK + GDN_V), CONV_K ** -0.5),
        "gdn_a_log": jnp.log(jax.random.uniform(ks[5], (DEPTH, GDN_HEADS), f32, 1.0, 16.0)),
        "gdn_dt_bias": dt + jnp.log(-jnp.expm1(-dt)),
        "gdn_norm_w": gain(ks[6], (DEPTH, GDN_DV)),
        "gla_gate_w2": nrm(ks[8], (DEPTH, GLA_GATE_RANK, GLA_QK), GLA_GATE_RANK ** -0.5),
        "gla_gate_b": nrm(ks[9], (DEPTH, GLA_QK), 0.01),
        "gla_norm_w": gain(ks[10], (DEPTH, GLA_DV)),
        "w_out": nrm(ks[11], (DEPTH, MIX_WIDTH, D_MODEL), MIX_WIDTH ** -0.5),
        "ffn_norm_w": gain(ks[12], (DEPTH, D_MODEL)),
        "w_gate": nrm(ks[13], (DEPTH, D_MODEL, D_FF), D_MODEL ** -0.5),
        "w_up": nrm(ks[14], (DEPTH, D_MODEL, D_FF), D_MODEL ** -0.5),
        "w_down": nrm(ks[15], (DEPTH, D_FF, D_MODEL), D_FF ** -0.5),
        "final_norm_w": gain(ks[16], (D_MODEL,)),
    }


def reference(x, meta_tokens, attn_norm_w, w_in, gdn_conv_w, gdn_a_log, gdn_dt_bias, gdn_norm_w,
              gla_gate_w2, gla_gate_b, gla_norm_w, w_out, ffn_norm_w, w_gate, w_up, w_down, final_norm_w):
    f32 = jnp.float32
    bsz = x.shape[0]
    meta = jnp.broadcast_to(meta_tokens.astype(x.dtype)[None], (bsz, N_META, D_MODEL))
    h = jnp.concatenate([meta, x], axis=1)
    L = h.shape[1]
    for layer in range(DEPTH):
        n = rms_norm(h, attn_norm_w[layer])
        proj = n @ w_in[layer]
        (gdn_qkv, gdn_z, gdn_a, gdn_b, gla_q, gla_k, gla_v, gla_r, gla_lr) = jnp.split(proj, IN_OFFSETS, axis=-1)

        qkv = causal_short_conv(gdn_qkv, gdn_conv_w[layer])
        q, k, v = jnp.split(qkv, (GDN_QK, 2 * GDN_QK), axis=-1)
        q = l2_normalize(q.reshape(bsz, L, GDN_HEADS, GDN_DK)) * (GDN_DK ** -0.5)
        k = l2_normalize(k.reshape(bsz, L, GDN_HEADS, GDN_DK))
        v = v.reshape(bsz, L, GDN_HEADS, GDN_DV)
        beta = jax.nn.sigmoid(gdn_b.astype(f32))
        g = -jnp.exp(gdn_a_log[layer].astype(f32)) * jax.nn.softplus(
            gdn_a.astype(f32) + gdn_dt_bias[layer].astype(f32))
        o_gdn = gated_delta_rule(q, k, v, beta, g)
        o_gdn = rms_norm(o_gdn, gdn_norm_w[layer]) * jax.nn.silu(gdn_z.reshape(bsz, L, GDN_HEADS, GDN_DV))

        gq = gla_q.reshape(bsz, L, GLA_HEADS, GLA_DK) * (GLA_DK ** -0.5)
        gk = gla_k.reshape(bsz, L, GLA_HEADS, GLA_DK)
        gv = gla_v.reshape(bsz, L, GLA_HEADS, GLA_DV)
        log_a = jax.nn.log_sigmoid((gla_lr @ gla_gate_w2[layer] + gla_gate_b[layer]).astype(f32)) / GLA_GATE_NORMALIZER
        o_gla = gla_chunked(gq, gk, gv, log_a.reshape(bsz, L, GLA_HEADS, GLA_DK))
        o_gla = rms_norm(o_gla, gla_norm_w[layer]) * jax.nn.silu(gla_r.reshape(bsz, L, GLA_HEADS, GLA_DV))

        mixed = jnp.concatenate([o_gdn.reshape(bsz, L, GDN_V), o_gla.reshape(bsz, L, GLA_V)], axis=-1)
        h = h + mixed @ w_out[layer]

        n = rms_norm(h, ffn_norm_w[layer])
        h = h + (jax.nn.silu(n @ w_gate[layer]) * (n @ w_up[layer])) @ w_down[layer]
    return rms_norm(h[:, N_META:], final_norm_w)
```

```python
import numpy as np
import ml_dtypes
from contextlib import ExitStack
import concourse.bass as bass
import concourse.mybir as mybir
from concourse.bass_utils import run_bass_kernel_spmd

F32 = mybir.dt.float32
BF16 = mybir.dt.bfloat16
ALU = mybir.AluOpType
AF = mybir.ActivationFunctionType
AX = mybir.AxisListType
EPS = 1e-6
T = 256
N_META = 16


class Prog:
    ENG = ("pe", "act", "dve", "pool", "sp")

    def __init__(self, nc):
        self.nc = nc
        self.ops = {e: [] for e in self.ENG}
        self.writers = {}
        self.readers = {}
        self.dma_count = {}
        self.out_dma = []
        self.dbg = False

    def _deps(self, eng, reads, writes):
        deps = {}

        def add(src, ref, raw=False):
            if src == eng and not (raw and eng != "pe"):
                return
            old = deps.get(src)
            if old is None or ref[2] > old[2]:
                deps[src] = ref
        for k in reads:
            for src, ref in self.writers.get(k, {}).items():
                add(src, ref, True)
            if isinstance(k, tuple) and k[0] == "ps":
                for src, ref in self.readers.get(k, {}).items():
                    add(src, ref, True)
        for k in writes:
            for src, ref in self.writers.get(k, {}).items():
                add(src, ref)
            for src, ref in self.readers.get(k, {}).items():
                add(src, ref)
        return list(deps.values())

    def op(self, eng, fn, reads=(), writes=()):
        deps = self._deps(eng, reads, writes)
        idx = len(self.ops[eng])
        self.ops[eng].append(dict(fn=fn, deps=deps, sig=False, dma=None))
        ref = ("eng", eng, idx)
        for k in reads:
            self.readers.setdefault(k, {})[eng] = ref
        for k in writes:
            self.writers.setdefault(k, {})[eng] = ref
        return ref

    def dma(self, fn, slot, reads=(), writes=(), queue="sp", is_output=False, n=1):
        deps = self._deps("dma:" + slot, reads, writes)
        cnt = self.dma_count.get(slot, 0) + n
        self.dma_count[slot] = cnt
        self.ops[queue].append(dict(fn=fn, deps=deps, sig=False, dma=(slot, cnt)))
        ref = ("dma", slot, cnt)
        src = "dma:" + slot
        for k in reads:
            self.readers.setdefault(k, {})[src] = ref
        for k in writes:
            self.writers.setdefault(k, {})[src] = ref
        if is_output:
            self.out_dma.append((slot, cnt))
        return ref

    def emit(self):
        nc = self.nc
        for e in self.ENG:
            for o in self.ops[e]:
                for d in o["deps"]:
                    if d[0] == "eng":
                        self.ops[d[1]][d[2]]["sig"] = True
        sigidx = {}
        for e in self.ENG:
            c = 0
            for i, o in enumerate(self.ops[e]):
                if o["sig"]:
                    c += 1
                    sigidx[(e, i)] = c
        with ExitStack() as st:
            esem = {e: st.enter_context(nc.semaphore("s_" + e)) for e in self.ENG}
            dsem = {s: st.enter_context(nc.semaphore("d_" + s)) for s in self.dma_count}
            block = st.enter_context(nc.Block())

            def run(ename, eng):
                waited = {}
                for i, o in enumerate(self.ops[ename]):
                    for d in o["deps"]:
                        if d[0] == "eng":
                            key = ("e", d[1]); val = sigidx[(d[1], d[2])]; sem = esem[d[1]]
                        else:
                            key = ("d", d[1]); val = 16 * d[2]; sem = dsem[d[1]]
                        if waited.get(key, 0) >= val:
                            continue
                        waited[key] = val
                        eng.wait_ge(sem, val)
                        if self.dbg:
                            print(ename, "   WAIT", key, val)
                    ins = o["fn"](eng)
                    if self.dbg:
                        try:
                            print(ename, i, "SIG" if o["sig"] else "", o["dma"], (ins[0] if isinstance(ins, (list, tuple)) else ins).concise()[:230])
                        except Exception as ex:
                            print(ename, i, "??", ex)
                    if o["dma"] is not None:
                        slot = o["dma"][0]
                        if isinstance(ins, (list, tuple)):
                            for x in ins:
                                x.then_inc(dsem[slot], 16)
                        else:
                            ins.then_inc(dsem[slot], 16)
                    elif o["sig"]:
                        ins.then_inc(esem[ename], 1)
                if ename == "sp":
                    final = {}
                    for slot, cnt in self.out_dma:
                        final[slot] = max(final.get(slot, 0), cnt)
                    for slot, cnt in final.items():
                        eng.wait_ge(dsem[slot], 16 * cnt)

            @block.tensor
            def _(eng):
                run("pe", eng)

            @block.scalar
            def _(eng):
                run("act", eng)

            @block.vector
            def _(eng):
                run("dve", eng)

            @block.gpsimd
            def _(eng):
                run("pool", eng)

            @block.sync
            def _(eng):
                run("sp", eng)


def make_cfg(D, HG, HL, DFF, NS, NF):
    c = dict(D=D, HG=HG, HL=HL, DFF=DFF, NS=NS, NF=NF)
    QK = HG * 128
    LQK = HL * 128
    LV = HL * 256
    c.update(QK=QK, LQK=LQK, LV=LV, MIX=QK + LV, KC=D // 128, KCM=(QK + LV) // 128, KCF=DFF // 128)
    o = {}
    o["qkv"] = 0
    o["z"] = 3 * QK
    o["a"] = 4 * QK
    o["b"] = 4 * QK + HG
    o["gq"] = 4 * QK + 2 * HG
    o["gk"] = o["gq"] + LQK
    o["gv"] = o["gk"] + LQK
    o["r"] = o["gv"] + LV
    o["lr"] = o["r"] + LV
    c["off"] = o
    c["DIN"] = o["lr"] + 16
    return c


def build(cfg):
    D, HG, HL, DFF, NS, NF = cfg["D"], cfg["HG"], cfg["HL"], cfg["DFF"], cfg["NS"], cfg["NF"]
    QK, LQK, LV, MIX, KC, KCM, KCF, DIN = (cfg[k] for k in ("QK", "LQK", "LV", "MIX", "KC", "KCM", "KCF", "DIN"))
    off = cfg["off"]
    NT = NS + NF
    GH = min(4, HG)
    NG = HG // GH
    GL = min(2, HL)
    NGL = HL // GL
    NB = T // 128
    NCH = T // 64

    nc = bass.Bass("TRN2", target_bir_lowering=False)

    def din(name, shape, dt=F32):
        return nc.dram_tensor(name, list(shape), dt, kind="ExternalInput").ap()

    xT_d = din("xT", [D, NT * T])
    w_in_d = din("w_in", [D, DIN])
    w_out_d = din("w_out", [MIX, D])
    w_gate_d = din("w_gate", [D, DFF])
    w_up_d = din("w_up", [D, DFF])
    w_down_d = din("w_down", [DFF, D])
    nw_d = din("nw", [128, 3, KC])
    cw_d = din("cw", [128, 3 * HG, 4])
    hb_d = din("hb", [2, HG])
    gnw_d = din("gnw", [128, 3])
    w2e_d = din("w2e", [128, LQK])
    cf_d = din("cf", [128, 3, 128])
    cb_d = din("cb", [128, 6, 128], BF16)
    out_d = nc.dram_tensor("outT", [D, NF * T], F32, kind="ExternalOutput").ap()
    wi_b = nc.dram_tensor("wi_b", [D, DIN], BF16, kind="Internal").ap()
    wo_b = nc.dram_tensor("wo_b", [MIX, D], BF16, kind="Internal").ap()
    wg_b = nc.dram_tensor("wg_b", [D, DFF], BF16, kind="Internal").ap()
    wu_b = nc.dram_tensor("wu_b", [D, DFF], BF16, kind="Internal").ap()
    wd_b = nc.dram_tensor("wd_b", [DFF, D], BF16, kind="Internal").ap()

    P = Prog(nc)
    P.dbg = bool(cfg.get('dbg'))
    st = ExitStack()
    with st:
        def sb(name, shape, dt=F32):
            return st.enter_context(nc.sbuf_tensor("sb_" + name, list(shape), dt))

        xh = sb("xh", [128, KC, T])
        nT = sb("nT", [128, KC, T], BF16)
        qkv = sb("qkv", [128, 3 * HG, T], BF16)
        zs = sb("zs", [128, HG, T], BF16)
        gqk = sb("gqk", [128, 2 * HL, T], BF16)
        rs = sb("rs", [128, 2 * HL, T], BF16)
        glav = sb("glav", [128, NB, LV], BF16)
        mixT = sb("mixT", [128, KCM, T], BF16)
        actT = sb("actT", [128, KCF, T], BF16)
        wb = [sb("wb%d" % i, [128, 16, 512], BF16) for i in range(2)]
        wsm = sb("wsm", [128, KC, 32], BF16)
        nw = sb("nw", [128, 3, KC])
        cw = sb("cw", [128, 3 * HG, 4])
        gnw = sb("gnw", [128, 3])
        cf = sb("cf", [128, 3, 128])
        cb = sb("cb", [128, 6, 128], BF16)
        hbr = sb("hbr", [64, 2, HG])
        negA = sb("negA", [64, HG])
        halo = sb("halo", [128, 3 * HG, 4], BF16)
        pre = [sb("pre%d" % i, [128, T + 4], BF16) for i in range(2)]
        ybuf = [sb("ybuf%d" % i, [128, T]) for i in range(2)]
        sqb = [sb("sqb%d" % i, [128, T], BF16) for i in range(2)]
        lnb = [sb("lnb%d" % i, [128, T]) for i in range(2)]
        rstd = sb("rstd", [128, T])
        sgb = [sb("sgb%d" % i, [128, T]) for i in range(2)]
        ost = [sb("ost%d" % i, [128, T]) for i in range(2)]
        lrT = sb("lrT", [128, T], BF16); w2eb = sb("w2eb", [128, LQK], BF16)
        ab = sb("ab", [64, NCH, 2 * HG])
        g_xg = sb("g_xg", [64, NCH, HG]); g_g = sb("g_g", [64, NCH, HG]); g_en = sb("g_en", [64, NCH, HG]); g_beta = sb("g_beta", [64, NCH, HG])
        g_gc = sb("g_gc", [64, HG]); g_eg = sb("g_eg", [64, HG]); g_egl = sb("g_egl", [128, HG]); g_dgl = sb("g_dgl", [64, HG])
        g_kd = sb("g_kd", [64, HG]); g_sbe = sb("g_sbe", [64, HG])
        g_LG = sb("g_LG", [64, HG, 64], BF16); g_LGl = sb("g_LGl", [64, HG, 64], BF16); g_gh = sb("g_gh", [64, NCH, HG], BF16); g_gl2 = sb("g_gl2", [64, NCH, HG], BF16); g_dec = sb("g_dec", [64, HG, 64]); g_dmS = sb("g_dmS", [64, HG, 64]); g_dmI = sb("g_dmI", [64, HG, 64])
        g_A1 = sb("g_A1", [64, HG, 64]); g_A = sb("g_A", [64, HG, 64], BF16); g_B = sb("g_B", [64, HG, 64], BF16)
        g_Pm = sb("g_Pm", [64, HG, 64], BF16); g_PT = sb("g_PT", [64, HG, 64], BF16)
        g_Pk = [sb("g_Pk%d" % i, [64, HG, 64], BF16) for i in range(2)]
        g_Qk = [sb("g_Qk%d" % i, [64, HG, 64], BF16) for i in range(2)]
        g_Y = [sb("g_Y%d" % i, [64, HG, 64], BF16) for i in range(2)]
        g_kbe = sb("g_kbe", [64, HG, 128], BF16); g_kdec = sb("g_kdec", [64, HG, 128], BF16); g_vb = sb("g_vb", [64, HG, 128], BF16)
        g_u = sb("g_u", [64, HG, 128]); g_wT = sb("g_wT", [128, HG, 64], BF16); g_vn = sb("g_vn", [64, HG, 128], BF16)
        g_S = sb("g_S", [128, HG, 128]); g_Sbf = sb("g_Sbf", [128, HG, 128], BF16)
        g_o = sb("g_o", [64, HG, 128]); g_sq = sb("g_sq", [128, 1024]); g_on = sb("g_on", [64, HG, 128], BF16)
        g_ss = sb("g_ss", [64, HG]); g_rs = sb("g_rs", [64, HG])
        l_la = sb("l_la", [128, HL, 128]); l_eb = sb("l_eb", [128, HL, 128]); l_enb = sb("l_enb", [128, HL, 128]); l_ekl = sb("l_ekl", [128, HL, 128])
        l_bl = sb("l_bl", [128, HL]); l_lah = [sb("l_lah%d" % i, [128, HL, 128], BF16) for i in range(NB)]; l_lal = [sb("l_lal%d" % i, [128, HL, 128], BF16) for i in range(NB)]
        l_Qt = sb("l_Qt", [128, HL, 128], BF16); l_Kt = sb("l_Kt", [128, HL, 128], BF16); l_KhT = sb("l_KhT", [128, HL, 128], BF16)
        l_sc = sb("l_sc", [128, HL, 128], BF16); l_Kh = sb("l_Kh", [128, HL, 128], BF16)
        l_S = sb("l_S", [128, HL, 256]); l_Sbf = sb("l_Sbf", [128, HL, 256], BF16)
        l_on = sb("l_on", [128, HL, 256], BF16); l_ss = sb("l_ss", [128, HL]); l_rs = sb("l_rs", [128, HL])
        ps = st.enter_context(nc.psum_tensor("ps", [128, 8 * 512], F32))
        psb = ps[:].bitcast(BF16)

        def PSF(b, p0, p1, c0, c1):
            return ps[p0:p1, b * 512 + c0: b * 512 + c1]

        def PSB(b, p0, p1, c0, c1):
            return psb[p0:p1, b * 1024 + c0: b * 1024 + c1]

        U_ = cf[:, 0, :]; Ls = cf[:, 1, :]; mI = cf[:, 2, :]
        identb = cb[:, 0, :]; onesb = cb[:, 1, :]; ones128b = cb[:, 2, :]; Ub = cb[:, 3, :]; Un16b = cb[:, 4, :]; Lsb = cb[:, 5, :]

        def TT(eng, out, in0, in1, op, reads, writes):
            P.op(eng, lambda e: e.tensor_tensor(out=out, in0=in0, in1=in1, op=op), reads, writes)

        def TS(eng, out, in0, s1, s2, op0, op1, reads, writes):
            if op1 is None:
                P.op(eng, lambda e: e.tensor_scalar(out=out, in0=in0, scalar1=s1, scalar2=None, op0=op0), reads, writes)
            else:
                P.op(eng, lambda e: e.tensor_scalar(out=out, in0=in0, scalar1=s1, scalar2=s2, op0=op0, op1=op1), reads, writes)

        def STT(eng, out, in0, scalar, in1, op0, op1, reads, writes):
            P.op(eng, lambda e: e.scalar_tensor_tensor(out=out, in0=in0, scalar=scalar, in1=in1, op0=op0, op1=op1), reads, writes)

        def ACT(out, in_, func, reads, writes, scale=None, bias=None):
            kw = {}
            if scale is not None:
                kw["scale"] = scale
            if bias is not None:
                kw["bias"] = bias
            P.op("act", lambda e: e.activation(out=out, in_=in_, func=func, **kw), reads, writes)

        def CP(eng, out, in_, reads, writes):
            if eng == "act":
                ACT(out, in_, AF.Copy, reads, writes)
            else:
                P.op(eng, lambda e: e.tensor_copy(out=out, in_=in_), reads, writes)

        def MM(out, lhsT, rhs, start, stop, reads, writes):
            P.op("pe", lambda e: e.matmul(out, lhsT=lhsT, rhs=rhs, start=start, stop=stop), reads, writes)

        def TR(out, in_, ident, reads, writes):
            P.op("pe", lambda e: e.transpose(out, in_, ident), reads, writes)

        def ATL():
            return

        def MS(eng, ap, val, writes):
            P.op(eng, lambda e: e.memset(ap, val), (), writes)

        cnt = dict(fb=0, wb=0, bank=0, alt=0, sq=0, pre=0, y=0, ln=0, sg=0, ost=0)

        def rot(name, n):
            v = cnt[name] % n
            cnt[name] += 1
            return v

        def ld(dst, src, key, slot):
            P.dma(lambda e: e.dma_start(out=dst, in_=src), slot, writes=[key])

        ld(nw[:], nw_d[:, :, :], "nw", "c")
        ld(cw[:], cw_d[:, :, :], "cw", "c")
        ld(gnw[:], gnw_d[:, :], "gnw", "c")
        ld(cf[:], cf_d[:, :, :], "cf", "c")
        ld(cb[:], cb_d[:, :, :], "cb", "c")
        ld(hbr[:, 0, :], hb_d[0:1, :].partition_broadcast(64), "hbr", "c")
        ld(hbr[:, 1, :], hb_d[1:2, :].partition_broadcast(64), "hbr", "c")
        NCONST = P.dma_count["c"]
        for k in ("nw", "cw", "gnw", "cf", "cb", "hbr"):
            P.writers[k] = {"dma:c": ("dma", "c", NCONST)}

        def cast_w(dst, src, rows, key, slot):
            r = 0
            while r < rows:
                n = min(128, rows - r)
                P.dma(lambda e, r=r, n=n: e.dma_start(out=dst[r:r + n, :], in_=src[r:r + n, :]), slot, writes=[key], queue="pool")
                r += n
            P.writers[key] = {"dma:" + slot: ("dma", slot, P.dma_count[slot])}

        cast_w(wi_b, w_in_d, D, "wi_b", "k0")
        cast_w(wo_b, w_out_d, MIX, "wo_b", "k1")
        cast_w(wg_b, w_gate_d, D, "wg_b", "k2")
        cast_w(wu_b, w_up_d, D, "wu_b", "k3")
        cast_w(wd_b, w_down_d, DFF, "wd_b", "k4")

        P.dma(lambda e: e.dma_start(out=wsm[:, :, 0:2 * HG], in_=wi_b[:, off["a"]:off["a"] + 2 * HG].rearrange("(k p) c -> p k c", p=128)),
              "c2", reads=["wi_b"], writes=["wsm"])
        P.dma(lambda e: e.dma_start(out=wsm[:, :, 16:32], in_=wi_b[:, off["lr"]:off["lr"] + 16].rearrange("(k p) c -> p k c", p=128)),
              "c2", reads=["wi_b"], writes=["wsm"])
        P.writers["wsm"] = {"dma:c2": ("dma", "c2", 2)}

        ACT(negA[:], hbr[:, 0, :], AF.Exp, ["hbr"], ["negA"])
        TS("dve", negA[:], negA[:], -1.0, None, ALU.mult, None, ["negA"], ["negA"])
        MS("pool", halo[:], 0.0, ["halo"])
        MS("pool", g_S[:], 0.0, ["g_S"])
        MS("pool", g_Sbf[:], 0.0, ["g_Sbf"])
        MS("pool", l_S[:], 0.0, ["l_S"])
        MS("pool", l_Sbf[:], 0.0, ["l_Sbf"])
        MS("pool", lrT[:], 1.0, ["lrT"])
        P.dma(lambda e: e.dma_start(out=g_sq[:, 0:LQK], in_=w2e_d[:, :]), "c3", writes=["g_sq"])
        CP("dve", w2eb[:], g_sq[:, 0:LQK], ["g_sq"], ["w2eb"])

        def load_w(src, key, k0, nk, c0, ncols):
            slot = rot("wb", 2)
            def fn(e):
                ins = []
                kk = 0
                while kk < nk:
                    n = min(4, nk - kk)
                    ins.append(e.dma_start(
                        out=wb[slot][:, kk:kk + n, 0:ncols],
                        in_=src[(k0 + kk) * 128:(k0 + kk + n) * 128, c0:c0 + ncols].rearrange("(k p) c -> p k c", p=128)))
                    kk += n
                return ins
            P.dma(fn, "w%d" % slot, reads=[key], writes=[("wb", slot)], n=(nk + 3) // 4)
            return slot

        pstate = {"gen": None}

        def pump():
            g = pstate["gen"]
            if g is not None:
                if next(g, "END") == "END":
                    pstate["gen"] = None

        def rms_stats(src, skey, Dn, KCn, bankset=None):
            bank = rot("bank", 4) if bankset is None else bankset[rot("fb", len(bankset))]
            for kc in range(KCn):
                s = rot("sq", 2)
                ACT(sqb[s][:], src[:, kc, :], AF.Square, [skey], [("sqb", s)])
                MM(PSF(bank, 0, 128, 0, T), onesb, sqb[s][:], kc == 0, kc == KCn - 1, [("sqb", s), "cb"], [("ps", bank)])
            l = rot("ln", 2)
            ACT(lnb[l][:], PSF(bank, 0, 128, 0, T), AF.Ln, [("ps", bank)], [("lnb", l)], scale=1.0 / Dn, bias=EPS)
            ACT(rstd[:], lnb[l][:], AF.Exp, [("lnb", l)], ["rstd"], scale=-0.5)

        def normalize(src, skey, which, dst, dkey, KCn):
            for kc in range(KCn):
                STT("dve", dst[:, kc, :], src[:, kc, :], nw[:, which, kc:kc + 1], rstd[:], ALU.mult, ALU.mult,
                    [skey, "nw", "rstd"], [dkey])

        def fm_group(src, key, k0, nk, c0, ncols, rhs_fn, rkeys, evac, first, last, banks=None):
            slot = load_w(src, key, k0, nk, c0, ncols)
            nm = ncols // 128
            if first:
                fm_group.banks = [rot("bank", 4) for _ in range(nm)] if banks is None else banks
            bks = fm_group.banks
            for m in range(nm):
                for kk in range(nk):
                    MM(PSF(bks[m], 0, 128, 0, T), wb[slot][:, kk, m * 128:(m + 1) * 128], rhs_fn(k0 + kk),
                       first and kk == 0, last and kk == nk - 1, [("wb", slot)] + rkeys, [("ps", bks[m])])
                if last:
                    evac(m, PSF(bks[m], 0, 128, 0, T), ("ps", bks[m]))

        def nT_rhs(kc):
            return nT[:, kc, :]

        def evac_qkv(rt, full):
            def f(m, pap, pkey):
                r = rt + m
                b = rot("pre", 2)
                y = rot("y", 2)
                CP("pool", pre[b][:, 0:3], halo[:, r, 0:3], ["halo"], [("pre", b)])
                ACT(pre[b][:, 3:3 + T], pap, AF.Copy, [pkey], [("pre", b)])
                CP("pool", halo[:, r, 0:3], pre[b][:, T:T + 3], [("pre", b)], ["halo"])
                TS("dve", ybuf[y][:], pre[b][:, 0:T], cw[:, r, 0:1], None, ALU.mult, None, [("pre", b), "cw"], [("y", y)])
                for i in range(1, 4):
                    STT("dve", ybuf[y][:], pre[b][:, i:i + T], cw[:, r, i:i + 1], ybuf[y][:], ALU.mult, ALU.add,
                        [("pre", b), "cw", ("y", y)], [("y", y)])
                ACT(qkv[:, r, :], ybuf[y][:], AF.Silu, [("y", y)], [("qkv", r)])
            return f

        def l2norm_rows(rows):
            ATL()
            for r in rows:
                if True:
                    isq = r < HG
                    s = rot("sq", 2)
                    TT("pool", sqb[s][:], qkv[:, r, :], qkv[:, r, :], ALU.mult, [("qkv", r)], [("sqb", s)])
                    bank = rot("bank", 4)
                    MM(PSF(bank, 0, 128, 0, T), ones128b if isq else onesb, sqb[s][:], True, True, [("sqb", s), "cb"], [("ps", bank)])
                    l = rot("ln", 2)
                    ACT(lnb[l][:], PSF(bank, 0, 128, 0, T), AF.Ln, [("ps", bank)], [("lnb", l)], bias=(128.0 * EPS if isq else EPS))
                    ACT(lnb[l][:], lnb[l][:], AF.Exp, [("lnb", l)], [("lnb", l)], scale=-0.5)
                    TT("pool", qkv[:, r, :], qkv[:, r, :], lnb[l][:], ALU.mult, [("qkv", r), ("lnb", l)], [("qkv", r)])

        def evac_silu(dst, dkey, rt):
            def f(m, pap, pkey):
                ACT(dst[:, rt + m, :], pap, AF.Silu, [pkey], [dkey])
            return f

        def evac_copy(dst, dkey, rt, scale=None):
            def f(m, pap, pkey):
                if scale is None:
                    CP("dve", dst[:, rt + m, :], pap, [pkey], [dkey])
                else:
                    ACT(dst[:, rt + m, :], pap, AF.Copy, [pkey], [dkey], scale=scale)
            return f

        def in_proj(full, need_q):
            def grp(c0, ncols_total, evac_maker):
                done = 0
                while done < ncols_total:
                    n = min(512, ncols_total - done)
                    k0 = 0
                    while k0 < KC:
                        nk = min(16, KC - k0)
                        fm_group(wi_b, "wi_b", k0, nk, c0 + done, n, nT_rhs, ["nT"], evac_maker(done // 128),
                                 k0 == 0, k0 + nk == KC)
                        k0 += nk
                    done += n
            if need_q:
                grp(off["qkv"], QK, lambda r0: evac_qkv(r0, full))
            skip = cfg.get("skip", ())
            if "k" not in skip:
                grp(off["qkv"] + QK, QK, lambda r0: evac_qkv(HG + r0, full))
            if "v" not in skip:
                grp(off["qkv"] + 2 * QK, QK, lambda r0: evac_qkv(2 * HG + r0, full))
            if full:
                grp(off["z"], QK, lambda r0: evac_silu(zs, "zs", r0))
                grp(off["gq"], LQK, lambda r0: evac_copy(gqk, "gqk", r0, scale=128.0 ** -0.5))
            if "gk" not in skip:
                grp(off["gk"], LQK, lambda r0: evac_copy(gqk, "gqk", HL + r0))
            if full:
                grp(off["r"], LV, lambda r0: evac_silu(rs, "rs", r0))
            l2rows = (list(range(0, HG)) if full else []) + (list(range(HG, 2 * HG)) if "k" not in skip else [])
            l2norm_rows(l2rows)
            done = 0
            while done < LV and "gv" not in skip:
                n = min(512, LV - done)
                assert KC <= 16
                slot = load_w(wi_b, "wi_b", 0, KC, off["gv"] + done, n)
                for blk in range(NB):
                    bank = rot("bank", 4)
                    for kc in range(KC):
                        MM(PSF(bank, 0, 128, 0, n), nT[:, kc, blk * 128:(blk + 1) * 128], wb[slot][:, kc, 0:n],
                           kc == 0, kc == KC - 1, [("wb", slot), "nT"], [("ps", bank)])
                    CP("act" if blk % 2 else "dve", glav[:, blk, done:done + n], PSF(bank, 0, 128, 0, n), [("ps", bank)], [("glav", blk)])
                done += n
            for ci in range(NCH if "ab" not in skip else 0):
                bank = rot("bank", 4)
                for kc in range(KC):
                    MM(PSF(bank, 0, 64, 0, 2 * HG), nT[:, kc, ci * 64:(ci + 1) * 64], wsm[:, kc, 0:2 * HG],
                       kc == 0, kc == KC - 1, ["nT", "wsm"], [("ps", bank)])
                CP("dve", ab[:, ci, :], PSF(bank, 0, 64, 0, 2 * HG), [("ps", bank)], [("ab", ci)])
            bank = rot("bank", 4)
            for kc in range(KC):
                MM(PSF(bank, 0, 16, 0, T), wsm[:, kc, 16:32], nT[:, kc, :], kc == 0, kc == KC - 1, ["nT", "wsm"], [("ps", bank)])
            CP("dve", lrT[0:16, :], PSF(bank, 0, 16, 0, T), [("ps", bank)], ["lrT"])
            abk = [("ab", ci) for ci in range(NCH)]
            f2 = lambda t: t[:].rearrange("p c h -> p (c h)")
            TT("dve", g_xg[:], ab[:, :, 0:HG], bc_m(hbr[:, 1, :], NCH), ALU.add, abk + ["hbr"], ["g_xg"])
            ACT(f2(g_xg), f2(g_xg), AF.Exp, ["g_xg"], ["g_xg"])
            TS("dve", f2(g_xg), f2(g_xg), 1.0, None, ALU.add, None, ["g_xg"], ["g_xg"])
            ACT(f2(g_xg), f2(g_xg), AF.Ln, ["g_xg"], ["g_xg"])
            TT("dve", g_g[:], g_xg[:], bc_m(negA[:], NCH), ALU.mult, ["g_xg", "negA"], ["g_g"])
            CP("dve", g_en[:], ab[:, :, HG:2 * HG], abk, ["g_en"])
            ACT(f2(g_en), f2(g_en), AF.Exp, ["g_en"], ["g_en"], scale=-1.0)
            TS("dve", f2(g_en), f2(g_en), 1.0, None, ALU.add, None, ["g_en"], ["g_en"])
            P.op("dve", lambda e: e.reciprocal(out=f2(g_beta), in_=f2(g_en)), ["g_en"], ["g_beta"])
            CP("dve", g_gh[:], g_g[:], ["g_g"], ["g_gh"])
            TT("dve", g_gl2[:], g_g[:], g_gh[:], ALU.subtract, ["g_g", "g_gh"], ["g_gl2"])
            for blk in range(NB):
                gla_gate(blk)

        def bc_h(ap2, n):
            return ap2.unsqueeze(2).to_broadcast([ap2.shape[0], ap2.shape[1], n])

        def bc_m(ap2, h):
            return ap2.unsqueeze(1).to_broadcast([ap2.shape[0], h, ap2.shape[1]])

        def gdn_chunk(ci, full):
            cs = slice(ci * 64, ci * 64 + 64)
            gg = g_g[:, ci, :]; gbeta = g_beta[:, ci, :]; ggh = g_gh[:, ci, :]; ggl = g_gl2[:, ci, :]
            TT("dve", g_LG[:], bc_m(U_[0:64, 0:64], HG), bc_h(ggh, 64), ALU.mult, ["cf", "g_gh"], ["g_LG"])
            TT("dve", g_LGl[:], bc_m(U_[0:64, 0:64], HG), bc_h(ggl, 64), ALU.mult, ["cf", "g_gl2"], ["g_LGl"])
            pump()
            MM(PSF(7, 0, 64, 256, 256 + HG), Ub[0:64, 0:64], ggh, True, False, ["cb", "g_gh"], [("ps", 7)])
            MM(PSF(7, 0, 64, 256, 256 + HG), Ub[0:64, 0:64], ggl, False, True, ["cb", "g_gl2"], [("ps", 7)])
            MM(PSF(7, 0, 128, 272, 272 + HG), onesb[0:64, :], ggh, True, False, ["cb", "g_gh"], [("ps", 7)])
            MM(PSF(7, 0, 128, 272, 272 + HG), onesb[0:64, :], ggl, False, True, ["cb", "g_gl2"], [("ps", 7)])
            for h in range(HG):
                MM(PSF(6, 0, 64, h * 64, (h + 1) * 64), g_LG[:, h, :], Lsb[0:64, 0:64], True, False, ["g_LG", "cb"], [("ps", 6)])
                MM(PSF(6, 0, 64, h * 64, (h + 1) * 64), g_LGl[:, h, :], Lsb[0:64, 0:64], False, True, ["g_LGl", "cb"], [("ps", 6)])
            CP("dve", g_gc[:], PSF(7, 0, 64, 256, 256 + HG), [("ps", 7)], ["g_gc"])
            pump()
            ACT(g_eg[:], PSF(7, 0, 64, 256, 256 + HG), AF.Exp, [("ps", 7)], ["g_eg"])
            ACT(g_egl[:], PSF(7, 0, 128, 272, 272 + HG), AF.Exp, [("ps", 7)], ["g_egl"])
            TT("dve", g_dgl[:], PSF(7, 0, 64, 272, 272 + HG), g_gc[:], ALU.subtract, [("ps", 7), "g_gc"], ["g_dgl"])
            ACT(g_kd[:], g_dgl[:], AF.Exp, ["g_dgl"], ["g_kd"])
            TT("dve", g_sbe[:], gbeta, g_eg[:], ALU.mult, ["g_beta", "g_eg"], ["g_sbe"])
            pump()
            ACT(g_dec[:].rearrange("p h c -> p (h c)"), PSF(6, 0, 64, 0, HG * 64), AF.Exp, [("ps", 6)], ["g_dec"])
            pump()
            TT("pool", g_dmS[:], g_dec[:], bc_m(Ls[0:64, 0:64], HG), ALU.mult, ["g_dec", "cf"], ["g_dmS"])
            if full:
                TT("pool", g_dmI[:], g_dec[:], bc_m(mI[0:64, 0:64], HG), ALU.mult, ["g_dec", "cf"], ["g_dmI"])
            for h in range(HG):
                TR(PSB(2, 0, 64, h * 128, (h + 1) * 128), qkv[:, HG + h, cs], identb, [("qkv", HG + h), "cb"], [("ps", 2)])
            for h in range(HG):
                TR(PSB(3, 0, 64, h * 128, (h + 1) * 128), qkv[:, 2 * HG + h, cs], identb, [("qkv", 2 * HG + h), "cb"], [("ps", 3)])
            kt3 = PSB(2, 0, 64, 0, HG * 128).rearrange("p (h d) -> p h d", h=HG)
            pump()
            vt3 = PSB(3, 0, 64, 0, HG * 128).rearrange("p (h d) -> p h d", h=HG)
            TT("dve", g_kbe[:], kt3, bc_h(g_sbe[:], 128), ALU.mult, [("ps", 2), "g_sbe"], ["g_kbe"])
            TT("dve", g_kdec[:], kt3, bc_h(g_kd[:], 128), ALU.mult, [("ps", 2), "g_kd"], ["g_kdec"])
            TT("dve", g_vb[:], vt3, bc_h(gbeta, 128), ALU.mult, [("ps", 3), "g_beta"], ["g_vb"])
            pump()
            for h in range(HG):
                MM(PSF(4, 0, 64, h * 64, (h + 1) * 64), qkv[:, HG + h, cs], qkv[:, HG + h, cs], True, True, [("qkv", HG + h)], [("ps", 4)])
            if full:
                for h in range(HG):
                    MM(PSF(5, 0, 64, h * 64, (h + 1) * 64), qkv[:, h, cs], qkv[:, HG + h, cs], True, True, [("qkv", h), ("qkv", HG + h)], [("ps", 5)])
            G3 = PSF(4, 0, 64, 0, HG * 64).rearrange("p (h c) -> p h c", h=HG)
            pump()
            TT("dve", g_A1[:], G3, g_dmS[:], ALU.mult, [("ps", 4), "g_dmS"], ["g_A1"])
            pump()
            TT("pool", g_A[:], g_A1[:], bc_h(gbeta, 64), ALU.mult, ["g_A1", "g_beta"], ["g_A"])
            pump()
            if full:
                P3 = PSF(5, 0, 64, 0, HG * 64).rearrange("p (h c) -> p h c", h=HG)
                TT("dve", g_Pm[:], P3, g_dmI[:], ALU.mult, [("ps", 5), "g_dmI"], ["g_Pm"])
            for h in range(HG):
                TR(PSB(6, 0, 64, h * 64, (h + 1) * 64), g_A[:, h, :], identb[0:64, 0:64], ["g_A", "cb"], [("ps", 6)])
            if full:
                for h in range(HG):
                    TR(PSB(7, 0, 64, h * 64, (h + 1) * 64), g_Pm[:, h, :], identb[0:64, 0:64], ["g_Pm", "cb"], [("ps", 7)])
            B3 = PSB(6, 0, 64, 0, HG * 64).rearrange("p (h c) -> p h c", h=HG)
            pump()
            CP("act", g_B[:], B3, [("ps", 6)], ["g_B"])
            STT("dve", g_Y[0][:], B3, -1.0, bc_m(identb[0:64, 0:64], HG), ALU.mult, ALU.add, [("ps", 6), "cb"], [("g_Y", 0)])
            pump()
            if full:
                CP("act", g_PT[:], PSB(7, 0, 64, 0, HG * 64).rearrange("p (h c) -> p h c", h=HG), [("ps", 7)], ["g_PT"])
            Pp, Qp, Pk_key, Qk_key = g_A, g_B, "g_A", "g_B"
            yi = 0
            for lvl in range(1, 6):
                i = lvl % 2
                for h in range(HG):
                    MM(PSF(4, 0, 64, h * 64, (h + 1) * 64), Qp[:, h, :], Pp[:, h, :], True, True, [Pk_key, Qk_key], [("ps", 4)])
                if lvl < 5:
                    for h in range(HG):
                        MM(PSF(5, 0, 64, h * 64, (h + 1) * 64), Pp[:, h, :], Qp[:, h, :], True, True, [Pk_key, Qk_key], [("ps", 5)])
                CP("act", g_Pk[i][:].rearrange("p h c -> p (h c)"), PSF(4, 0, 64, 0, HG * 64), [("ps", 4)], [("g_Pk", i)])
                if lvl < 5:
                    CP("dve", g_Qk[i][:].rearrange("p h c -> p (h c)"), PSF(5, 0, 64, 0, HG * 64), [("ps", 5)], [("g_Qk", i)])
                pump()
                for h in range(HG):
                    MM(PSF(2, 0, 64, h * 64, (h + 1) * 64), g_Pk[i][:, h, :], g_Y[yi][:, h, :], True, True, [("g_Pk", i), ("g_Y", yi)], [("ps", 2)])
                TT("dve", g_Y[1 - yi][:].rearrange("p h c -> p (h c)"), PSF(2, 0, 64, 0, HG * 64), g_Y[yi][:].rearrange("p h c -> p (h c)"),
                   ALU.add, [("ps", 2), ("g_Y", yi)], [("g_Y", 1 - yi)])
                yi = 1 - yi
                pump()
                Pp, Qp, Pk_key, Qk_key = g_Pk[i], g_Qk[i], ("g_Pk", i), ("g_Qk", i)
            Y = g_Y[yi]
            Ykey = ("g_Y", yi)
            for gi in range(NG):
                for hh in range(GH):
                    h = gi * GH + hh
                    MM(PSF(2 + gi, 0, 64, hh * 128, (hh + 1) * 128), Y[:, h, :], g_vb[:, h, :], True, True, [Ykey, "g_vb"], [("ps", 2 + gi)])
            for h in range(HG):
                MM(PSF(6, 0, 128, h * 64, (h + 1) * 64), g_kbe[:, h, :], Y[:, h, :], True, True, [Ykey, "g_kbe"], [("ps", 6)])
            for gi in range(NG):
                CP("act", g_u[:, gi * GH:(gi + 1) * GH, :].rearrange("p h d -> p (h d)"), PSF(2 + gi, 0, 64, 0, GH * 128), [("ps", 2 + gi)], ["g_u"])
            CP("dve", g_wT[:].rearrange("p h c -> p (h c)"), PSF(6, 0, 128, 0, HG * 64), [("ps", 6)], ["g_wT"])
            pump()
            for gi in range(NG):
                for hh in range(GH):
                    h = gi * GH + hh
                    MM(PSF(4 + gi, 0, 64, hh * 128, (hh + 1) * 128), g_wT[:, h, :], g_Sbf[:, h, :], True, True, ["g_wT", "g_Sbf"], [("ps", 4 + gi)])
            for gi in range(NG):
                TT("dve", g_vn[:, gi * GH:(gi + 1) * GH, :].rearrange("p h d -> p (h d)"),
                   g_u[:, gi * GH:(gi + 1) * GH, :].rearrange("p h d -> p (h d)"), PSF(4 + gi, 0, 64, 0, GH * 128), ALU.subtract,
                   ["g_u", ("ps", 4 + gi)], ["g_vn"])
            pump()
            if full:
                for gi in range(NG):
                    for hh in range(GH):
                        h = gi * GH + hh
                        MM(PSF(4 + gi, 0, 64, hh * 128, (hh + 1) * 128), qkv[:, h, cs], g_Sbf[:, h, :], True, True, [("qkv", h), "g_Sbf"], [("ps", 4 + gi)])
                        MM(PSF(2 + gi, 0, 64, hh * 128, (hh + 1) * 128), g_PT[:, h, :], g_vn[:, h, :], True, True, ["g_PT", "g_vn"], [("ps", 2 + gi)])
            for gi in range(NG):
                for hh in range(GH):
                    h = gi * GH + hh
                    MM(PSF(6 + gi, 0, 128, hh * 128, (hh + 1) * 128), g_kdec[:, h, :], g_vn[:, h, :], True, True, ["g_kdec", "g_vn"], [("ps", 6 + gi)])
            if full:
                for gi in range(NG):
                    hs = slice(gi * GH, (gi + 1) * GH)
                    o1 = PSF(4 + gi, 0, 64, 0, GH * 128).rearrange("p (h d) -> p h d", h=GH)
                    o2 = PSF(2 + gi, 0, 64, 0, GH * 128).rearrange("p (h d) -> p h d", h=GH)
                    TT("dve", g_o[:, hs, :], o1, bc_h(g_eg[:, hs], 128), ALU.mult, [("ps", 4 + gi), "g_eg"], ["g_o"])
                    TT("dve", g_o[:, hs, :], g_o[:, hs, :], o2, ALU.add, [("ps", 2 + gi), "g_o"], ["g_o"])
            TT("pool", g_S[:], g_S[:], bc_h(g_egl[:], 128), ALU.mult, ["g_S", "g_egl"], ["g_S"])
            for gi in range(NG):
                hs = slice(gi * GH, (gi + 1) * GH)
                TT("dve", g_S[:, hs, :].rearrange("p h d -> p (h d)"), g_S[:, hs, :].rearrange("p h d -> p (h d)"),
                   PSF(6 + gi, 0, 128, 0, GH * 128), ALU.add, ["g_S", ("ps", 6 + gi)], ["g_S"])
            CP("act", g_Sbf[:], g_S[:], ["g_S"], ["g_Sbf"])
            pump()
            if full:
                ACT(g_sq[0:64, 0:HG * 128], g_o[:].rearrange("p h d -> p (h d)"), AF.Square, ["g_o"], ["g_sq"])
                P.op("dve", lambda e: e.tensor_reduce(out=g_ss[:], in_=g_sq[0:64, 0:HG * 128].rearrange("p (h d) -> p h d", h=HG), axis=AX.X, op=ALU.add),
                     ["g_sq"], ["g_ss"])
                ACT(g_rs[:], g_ss[:], AF.Ln, ["g_ss"], ["g_rs"], scale=1.0 / 128, bias=EPS)
                ACT(g_rs[:], g_rs[:], AF.Exp, ["g_rs"], ["g_rs"], scale=-0.5)
                TT("pool", g_on[:], g_o[:], bc_h(g_rs[:], 128), ALU.mult, ["g_o", "g_rs"], ["g_on"])
                for h in range(HG):
                    TR(PSB(4, 0, 128, h * 64, (h + 1) * 64), g_on[:, h, :], identb[0:64, 0:64], ["g_on", "cb"], [("ps", 4)])
                oT3 = PSB(4, 0, 128, 0, HG * 64).rearrange("p (h c) -> p h c", h=HG)
                STT("dve", mixT[:, 0:HG, cs], oT3, gnw[:, 0:1], zs[:, :, cs], ALU.mult, ALU.mult, [("ps", 4), "gnw", "zs"], ["mixT"])

            pump()

        def gla_gate(blk):
            bs = slice(blk * 128, blk * 128 + 128)
            MM(PSF(0, 0, 128, 0, LQK), lrT[:, bs], w2eb[:], True, True, ["lrT", "w2eb"], [("ps", 0)])
            la2 = l_la[:].rearrange("p h d -> p (h d)")
            ACT(la2, PSF(0, 0, 128, 0, LQK), AF.Exp, [("ps", 0)], ["l_la"], scale=-1.0)
            TS("dve", la2, la2, 1.0, None, ALU.add, None, ["l_la"], ["l_la"])
            ACT(la2, la2, AF.Ln, ["l_la"], ["l_la"])
            CP("dve", l_lah[blk][:], l_la[:], ["l_la"], [("l_lah", blk)])
            TT("dve", l_lal[blk][:], l_la[:], l_lah[blk][:], ALU.subtract, ["l_la", ("l_lah", blk)], [("l_lal", blk)])

        def gla_chunk(blk, full):
            bs = slice(blk * 128, blk * 128 + 128)
            if cfg.get('sub', 99) < 2.1:
                return
            for h in range(HL):
                MM(PSF(2, 0, 128, h * 128, (h + 1) * 128), l_lah[blk][:, h, :], Un16b, True, False, [("l_lah", blk), "cb"], [("ps", 2)])
                MM(PSF(2, 0, 128, h * 128, (h + 1) * 128), l_lal[blk][:, h, :], Un16b, False, True, [("l_lal", blk), "cb"], [("ps", 2)])
            pump()
            b3 = PSF(2, 0, 128, 0, HL * 128).rearrange("p (h i) -> p h i", h=HL)
            ACT(l_eb[:].rearrange("p h d -> p (h d)"), PSF(2, 0, 128, 0, HL * 128), AF.Exp, [("ps", 2)], ["l_eb"])
            if cfg.get("extra") == "one":
                return
            if cfg.get("extra") == "dvecp":
                CP("dve", l_enb[:].rearrange("p h d -> p (h d)"), PSF(2, 0, 128, 0, HL * 128), [("ps", 2)], ["l_enb"])
                return
            ACT(l_enb[:].rearrange("p h d -> p (h d)"), PSF(2, 0, 128, 0, HL * 128), AF.Exp, [("ps", 2)], ["l_enb"], scale=-1.0)
            if cfg.get('sub', 99) < 2.3:
                return
            CP("dve", l_bl[:], b3[:, :, 127], [("ps", 2)], ["l_bl"])
            if cfg.get('sub', 99) < 2.4:
                return
            for h in range(HL):
                ACT(l_ekl[:, h, :], PSF(2, 0, 128, h * 128, (h + 1) * 128), AF.Exp, [("ps", 2), "l_bl"], ["l_ekl"], scale=-1.0, bias=l_bl[:, h:h + 1])
            pump()
            if full:
                TT("pool", l_Qt[:], gqk[:, 0:HL, bs], l_eb[:], ALU.mult, ["gqk", "l_eb"], ["l_Qt"])
                TT("pool", l_Kt[:], gqk[:, HL:2 * HL, bs], l_enb[:], ALU.mult, ["gqk", "l_enb"], ["l_Kt"])
            TT("dve", l_KhT[:], gqk[:, HL:2 * HL, bs], l_ekl[:], ALU.mult, ["gqk", "l_ekl"], ["l_KhT"])
            pump()
            if full:
                for h in range(HL):
                    MM(PSF(3, 0, 128, h * 128, (h + 1) * 128), l_Kt[:, h, :], l_Qt[:, h, :], True, True, ["l_Kt", "l_Qt"], [("ps", 3)])
            for h in range(HL):
                TR(PSB(2, 0, 128, h * 128, (h + 1) * 128), l_KhT[:, h, :], identb, ["l_KhT", "cb"], [("ps", 2)])
            if full:
                TT("dve", l_sc[:], PSF(3, 0, 128, 0, HL * 128).rearrange("p (h i) -> p h i", h=HL), bc_m(U_, HL), ALU.mult,
                   [("ps", 3), "cf"], ["l_sc"])
            CP("act", l_Kh[:].rearrange("p h d -> p (h d)"), PSB(2, 0, 128, 0, HL * 128), [("ps", 2)], ["l_Kh"])
            pump()
            if cfg.get('sub', 99) < 4:
                return
            if full:
                for gi in range(NGL):
                    for hh in range(GL):
                        h = gi * GL + hh
                        MM(PSF(4 + gi, 0, 128, hh * 256, (hh + 1) * 256), l_Qt[:, h, :], l_Sbf[:, h, :], True, False, ["l_Qt", "l_Sbf"], [("ps", 4 + gi)])
                        MM(PSF(4 + gi, 0, 128, hh * 256, (hh + 1) * 256), l_sc[:, h, :], glav[:, blk, h * 256:(h + 1) * 256], False, True,
                           ["l_sc", ("glav", blk)], [("ps", 4 + gi)])
            for gi in range(NGL):
                for hh in range(GL):
                    h = gi * GL + hh
                    MM(PSF(6 + gi, 0, 128, hh * 256, (hh + 1) * 256), l_Kh[:, h, :], glav[:, blk, h * 256:(h + 1) * 256], True, True,
                       ["l_Kh", ("glav", blk)], [("ps", 6 + gi)])
            for gi in range(NGL):
                for hh in range(GL):
                    h = gi * GL + hh
                    STT("dve", l_S[:, h, :], l_S[:, h, :], l_eb[:, h, 127:128], PSF(6 + gi, 0, 128, hh * 256, (hh + 1) * 256), ALU.mult, ALU.add,
                        ["l_S", "l_eb", ("ps", 6 + gi)], ["l_S"])
            if cfg.get('sub', 99) < 5:
                return
            if full:
                for gi in range(NGL):
                    ACT(g_sq[:, gi * 512:gi * 512 + GL * 256], PSF(4 + gi, 0, 128, 0, GL * 256), AF.Square, [("ps", 4 + gi)], ["g_sq"])
                P.op("dve", lambda e: e.tensor_reduce(out=l_ss[:], in_=g_sq[:, 0:HL * 256].rearrange("p (h d) -> p h d", h=HL), axis=AX.X, op=ALU.add),
                     ["g_sq"], ["l_ss"])
                ACT(l_rs[:], l_ss[:], AF.Ln, ["l_ss"], ["l_rs"], scale=1.0 / 256, bias=EPS)
                ACT(l_rs[:], l_rs[:], AF.Exp, ["l_rs"], ["l_rs"], scale=-0.5)
                for gi in range(NGL):
                    hs = slice(gi * GL, (gi + 1) * GL)
                    TT("dve", l_on[:, hs, :], PSF(4 + gi, 0, 128, 0, GL * 256).rearrange("p (h d) -> p h d", h=GL), bc_h(l_rs[:, hs], 256), ALU.mult,
                       [("ps", 4 + gi), "l_rs"], ["l_on"])
            CP("act", l_Sbf[:], l_S[:], ["l_S"], ["l_Sbf"])
            pump()
            if full:
                for h in range(HL):
                    for half in range(2):
                        j = 2 * h + half
                        TR(PSB(3, 0, 128, j * 128, (j + 1) * 128), l_on[:, h, half * 128:(half + 1) * 128], identb, ["l_on", "cb"], [("ps", 3)])
                oT4 = PSB(3, 0, 128, 0, 2 * HL * 128).rearrange("p (h two i) -> p h two i", h=HL, two=2)
                rs4 = rs[:, :, bs].rearrange("p (h two) i -> p h two i", two=2)
                mx4 = mixT[:, HG:HG + 2 * HL, bs].rearrange("p (h two) i -> p h two i", two=2)
                for half in range(2):
                    STT("dve", mx4[:, :, half, :], oT4[:, :, half, :], gnw[:, 1 + half:2 + half], rs4[:, :, half, :], ALU.mult, ALU.mult,
                        [("ps", 3), "gnw", "rs"], ["mixT"])
            pump()

        actT_flat = actT[:].rearrange("p k t -> p (k t)")
        XA = actT_flat[:, 0:KC * T * 2].bitcast(F32).rearrange("p (k t) -> p k t", k=KC)

        def evac_res(rt0):
            def f(m, pap, pkey):
                TT("dve", xh[:, rt0 + m, :], xh[:, rt0 + m, :], pap, ALU.add, ["xh", pkey], ["xh"])
            return f

        def out_proj(ti):
            tcols = slice(ti * T, (ti + 1) * T)
            P.dma(lambda e: e.dma_start(out=xh[:], in_=xT_d[:, tcols].rearrange("(k p) t -> p k t", p=128)), "x2", writes=["xh"])
            done = 0
            while done < D:
                n = min(512, D - done)
                k0 = 0
                while k0 < KCM:
                    nk = min(16, KCM - k0)
                    fm_group(wo_b, "wo_b", k0, nk, done, n, lambda kc: mixT[:, kc, :], ["mixT"], evac_res(done // 128), k0 == 0, k0 + nk == KCM)
                    k0 += nk
                done += n

        def F_gen(ti):
            FB = (0, 1)
            rms_stats(xh, "xh", D, KC, bankset=FB)
            normalize(xh, "xh", 1, nT, "nT", KC)
            yield 1
            done = 0
            while done < DFF:
                n = min(512, DFF - done)
                nm = n // 128
                sl_g = load_w(wg_b, "wg_b", 0, KC, done, n)
                sl_u = load_w(wu_b, "wu_b", 0, KC, done, n)
                for m in range(nm):
                    for kc in range(KC):
                        MM(PSF(0, 0, 128, 0, T), wb[sl_g][:, kc, m * 128:(m + 1) * 128], nT[:, kc, :], kc == 0, kc == KC - 1,
                           [("wb", sl_g), "nT"], [("ps", 0)])
                    sg = rot("sg", 2)
                    ACT(sgb[sg][:], PSF(0, 0, 128, 0, T), AF.Silu, [("ps", 0)], [("sgb", sg)])
                    yield 1
                    for kc in range(KC):
                        MM(PSF(1, 0, 128, 0, T), wb[sl_u][:, kc, m * 128:(m + 1) * 128], nT[:, kc, :], kc == 0, kc == KC - 1,
                           [("wb", sl_u), "nT"], [("ps", 1)])
                    TT("dve", actT[:, done // 128 + m, :], sgb[sg][:], PSF(1, 0, 128, 0, T), ALU.mult, [("sgb", sg), ("ps", 1)], ["actT"])
                    yield 1
                done += n
            done = 0
            while done < D:
                n = min(256, D - done)
                nm = n // 128
                acc = [(m, 0) for m in range(nm)]
                k0 = 0
                while k0 < KCF:
                    nk = min(16, KCF - k0)
                    slot = load_w(wd_b, "wd_b", k0, nk, done, n)
                    for m in range(nm):
                        bk, co = acc[m]
                        for kk in range(nk):
                            MM(PSF(bk, 0, 128, co, co + T), wb[slot][:, kk, m * 128:(m + 1) * 128], actT[:, k0 + kk, :],
                               k0 == 0 and kk == 0, k0 + nk == KCF and kk == nk - 1, [("wb", slot), "actT"], [("ps", bk)])
                    k0 += nk
                    yield 1
                for m in range(nm):
                    bk, co = acc[m]
                    evac_res(done // 128)(m, PSF(bk, 0, 128, co, co + T), ("ps", bk))
                done += n
            rms_stats(xh, "xh", D, KC, bankset=FB)
            fcols = slice((ti - NS) * T, (ti - NS + 1) * T)
            for kc in range(KC):
                o = rot("ost", 2)
                STT("dve", ost[o][:], xh[:, kc, :], nw[:, 2, kc:kc + 1], rstd[:], ALU.mult, ALU.mult,
                    ["xh", "nw", "rstd"], [("ost", o)])
                P.dma(lambda e, o=o, kc=kc, fcols=fcols: e.dma_start(out=out_d[kc * 128:(kc + 1) * 128, fcols], in_=ost[o][:]),
                      "o%d" % o, reads=[("ost", o)], is_output=True)
                if kc % 4 == 3:
                    yield 1

        def drain():
            while pstate["gen"] is not None:
                pump()

        prev_full = None
        for ti in range(NT):
            if ti >= cfg.get("ntiles", 999):
                break
            full = ti >= NS
            tcols = slice(ti * T, (ti + 1) * T)
            P.dma(lambda e, tcols=tcols: e.dma_start(out=XA, in_=xT_d[:, tcols].rearrange("(k p) t -> p k t", p=128)),
                  "x", writes=["actT"])
            rms_stats(XA, "actT", D, KC)
            normalize(XA, "actT", 0, nT, "nT", KC)
            in_proj(full, full or ti == NS - 1)
            if prev_full is not None:
                out_proj(prev_full)
                pstate["gen"] = F_gen(prev_full)
            for blk in range(NB):
                gla_chunk(blk, full)
                gdn_chunk(2 * blk, full)
                gdn_chunk(2 * blk + 1, full)
            drain()
            prev_full = ti if full else None
        if prev_full is not None:
            out_proj(prev_full)
            pstate["gen"] = F_gen(prev_full)
            drain()
        P.emit()
    return nc


def host_consts():
    k = np.arange(128)
    U = (k[:, None] <= k[None, :]).astype(np.float32)
    Ls = (k[:, None] > k[None, :]).astype(np.float32)
    mI = (k[:, None] >= k[None, :]).astype(np.float32)
    cf = np.stack([U, Ls, mI], axis=1).astype(np.float32)
    bf = ml_dtypes.bfloat16
    cb = np.stack([np.eye(128, dtype=np.float32), np.ones((128, 128), np.float32), np.full((128, 128), 128.0, np.float32), U, -U / 16.0, Ls], axis=1).astype(bf)
    return np.ascontiguousarray(cf), np.ascontiguousarray(cb)


def prep_shared(cfg, inp):
    D, HG, HL, KC = cfg["D"], cfg["HG"], cfg["HL"], cfg["KC"]
    f = lambda a: np.ascontiguousarray(np.asarray(a, dtype=np.float32))
    def pk(v):
        return f(v).reshape(KC, 128).T
    nw = np.ascontiguousarray(np.stack([pk(inp["attn_norm_w"][0]), pk(inp["ffn_norm_w"][0]), pk(inp["final_norm_w"])], axis=1))
    cwf = f(inp["gdn_conv_w"][0])
    cw = np.ascontiguousarray(cwf.reshape(4, 3 * HG, 128).transpose(2, 1, 0))
    hb = np.ascontiguousarray(np.stack([f(inp["gdn_a_log"][0]), f(inp["gdn_dt_bias"][0])], axis=0))
    glw = f(inp["gla_norm_w"][0])
    gnw = np.ascontiguousarray(np.stack([f(inp["gdn_norm_w"][0]), glw[0:128], glw[128:256]], axis=1))
    w2e = np.ascontiguousarray(np.concatenate([f(inp["gla_gate_w2"][0]), f(inp["gla_gate_b"][0])[None, :], np.zeros((111, inp["gla_gate_b"].shape[-1]), np.float32)], axis=0))
    cf, cb = host_consts()
    return dict(w_in=f(inp["w_in"][0]), w_out=f(inp["w_out"][0]), w_gate=f(inp["w_gate"][0]), w_up=f(inp["w_up"][0]),
                w_down=f(inp["w_down"][0]), nw=nw, cw=cw, hb=hb, gnw=gnw, w2e=w2e, cf=cf, cb=cb)


def core_sequences(cfg, x, meta, S):
    D, NS, NF = cfg["D"], cfg["NS"], cfg["NF"]
    B = x.shape[0]
    half = S // 2
    assert NF * T == half and NS * T >= half + N_META
    seqs = []
    for b in range(B):
        for p in range(2):
            seq = np.zeros((NS * T + NF * T, D), np.float32)
            if p == 0:
                seq[NS * T - N_META:NS * T] = meta
                seq[NS * T:] = x[b, :half]
            else:
                seq[NS * T - half - N_META:NS * T - half] = meta
                seq[NS * T - half:NS * T] = x[b, :half]
                seq[NS * T:] = x[b, half:]
            seqs.append((b, p, np.ascontiguousarray(seq.T)))
    return seqs


def kernel(x, meta_tokens, attn_norm_w, w_in, gdn_conv_w, gdn_a_log, gdn_dt_bias, gdn_norm_w,
           gla_gate_w2, gla_gate_b, gla_norm_w, w_out, ffn_norm_w, w_gate, w_up, w_down, final_norm_w):
    x = np.asarray(x, dtype=np.float32)
    B, S, D = x.shape
    half = S // 2
    NF = half // T
    NS = -(-(half + N_META) // T)
    cfg = make_cfg(D, 8, 4, w_gate.shape[-1], NS, NF)
    inp = dict(attn_norm_w=attn_norm_w, w_in=w_in, gdn_conv_w=gdn_conv_w, gdn_a_log=gdn_a_log, gdn_dt_bias=gdn_dt_bias,
               gdn_norm_w=gdn_norm_w, gla_gate_w2=gla_gate_w2, gla_gate_b=gla_gate_b, gla_norm_w=gla_norm_w, w_out=w_out,
               ffn_norm_w=ffn_norm_w, w_gate=w_gate, w_up=w_up, w_down=w_down, final_norm_w=final_norm_w)
    inp = {k: np.asarray(v) for k, v in inp.items()}
    shared = prep_shared(cfg, inp)
    seqs = core_sequences(cfg, x, np.asarray(meta_tokens, dtype=np.float32), S)
    nc = build(cfg)
    in_maps = [dict(shared, xT=s[2]) for s in seqs]
    res = run_bass_kernel_spmd(nc, in_maps, core_ids=list(range(len(in_maps))))
    out = np.empty((B, S, D), np.float32)
    for (b, p, _), r in zip(seqs, res.results):
        out[b, p * half:(p + 1) * half, :] = r["outT"].T
    return out
```

```python
import numpy as np
import ml_dtypes
from contextlib import ExitStack
import concourse.bass as bass
import concourse.mybir as mybir
from concourse.bass_utils import run_bass_kernel_spmd

F32 = mybir.dt.float32
BF16 = mybir.dt.bfloat16
ALU = mybir.AluOpType
AF = mybir.ActivationFunctionType
AX = mybir.AxisListType
EPS = 1e-6
T = 256
N_META = 16


class Prog:
    ENG = ("pe", "act", "dve", "pool", "sp")

    def __init__(self, nc):
        self.nc = nc
        self.ops = {e: [] for e in self.ENG}
        self.writers = {}
        self.readers = {}
        self.dma_count = {}
        self.out_dma = []
        self.dbg = False

    def _deps(self, eng, reads, writes):
        deps = {}

        def add(src, ref, raw=False):
            if src == eng and not (raw and eng != "pe"):
                return
            old = deps.get(src)
            if old is None or ref[2] > old[2]:
                deps[src] = ref
        for k in reads:
            for src, ref in self.writers.get(k, {}).items():
                add(src, ref, True)
            if isinstance(k, tuple) and k[0] == "ps":
                for src, ref in self.readers.get(k, {}).items():
                    add(src, ref, True)
        for k in writes:
            for src, ref in self.writers.get(k, {}).items():
                add(src, ref)
            for src, ref in self.readers.get(k, {}).items():
                add(src, ref)
        return list(deps.values())

    def op(self, eng, fn, reads=(), writes=()):
        deps = self._deps(eng, reads, writes)
        idx = len(self.ops[eng])
        self.ops[eng].append(dict(fn=fn, deps=deps, sig=False, dma=None))
        ref = ("eng", eng, idx)
        for k in reads:
            self.readers.setdefault(k, {})[eng] = ref
        for k in writes:
            self.writers.setdefault(k, {})[eng] = ref
        return ref

    def dma(self, fn, slot, reads=(), writes=(), queue="sp", is_output=False, n=1):
        deps = self._deps("dma:" + slot, reads, writes)
        cnt = self.dma_count.get(slot, 0) + n
        self.dma_count[slot] = cnt
        self.ops[queue].append(dict(fn=fn, deps=deps, sig=False, dma=(slot, cnt)))
        ref = ("dma", slot, cnt)
        src = "dma:" + slot
        for k in reads:
            self.readers.setdefault(k, {})[src] = ref
        for k in writes:
            self.writers.setdefault(k, {})[src] = ref
        if is_output:
            self.out_dma.append((slot, cnt))
        return ref

    def emit(self):
        nc = self.nc
        for e in self.ENG:
            for o in self.ops[e]:
                for d in o["deps"]:
                    if d[0] == "eng":
                        self.ops[d[1]][d[2]]["sig"] = True
        sigidx = {}
        for e in self.ENG:
            c = 0
            for i, o in enumerate(self.ops[e]):
                if o["sig"]:
                    c += 1
                    sigidx[(e, i)] = c
        with ExitStack() as st:
            esem = {e: st.enter_context(nc.semaphore("s_" + e)) for e in self.ENG}
            dsem = {s: st.enter_context(nc.semaphore("d_" + s)) for s in self.dma_count}
            block = st.enter_context(nc.Block())

            def run(ename, eng):
                waited = {}
                for i, o in enumerate(self.ops[ename]):
                    for d in o["deps"]:
                        if d[0] == "eng":
                            key = ("e", d[1]); val = sigidx[(d[1], d[2])]; sem = esem[d[1]]
                        else:
                            key = ("d", d[1]); val = 16 * d[2]; sem = dsem[d[1]]
                        if waited.get(key, 0) >= val:
                            continue
                        waited[key] = val
                        eng.wait_ge(sem, val)
                        if self.dbg:
                            print(ename, "   WAIT", key, val)
                    ins = o["fn"](eng)
                    if self.dbg:
                        try:
                            print(ename, i, "SIG" if o["sig"] else "", o["dma"], (ins[0] if isinstance(ins, (list, tuple)) else ins).concise()[:230])
                        except Exception as ex:
                            print(ename, i, "??", ex)
                    if o["dma"] is not None:
                        slot = o["dma"][0]
                        if isinstance(ins, (list, tuple)):
                            for x in ins:
                                x.then_inc(dsem[slot], 16)
                        else:
                            ins.then_inc(dsem[slot], 16)
                    elif o["sig"]:
                        ins.then_inc(esem[ename], 1)
                if ename == "sp":
                    final = {}
                    for slot, cnt in self.out_dma:
                        final[slot] = max(final.get(slot, 0), cnt)
                    for slot, cnt in final.items():
                        eng.wait_ge(dsem[slot], 16 * cnt)

            @block.tensor
            def _(eng):
                run("pe", eng)

            @block.scalar
            def _(eng):
                run("act", eng)

            @block.vector
            def _(eng):
                run("dve", eng)

            @block.gpsimd
            def _(eng):
                run("pool", eng)

            @block.sync
            def _(eng):
                run("sp", eng)


def make_cfg(D, HG, HL, DFF, NS, NF):
    c = dict(D=D, HG=HG, HL=HL, DFF=DFF, NS=NS, NF=NF)
    QK = HG * 128
    LQK = HL * 128
    LV = HL * 256
    c.update(QK=QK, LQK=LQK, LV=LV, MIX=QK + LV, KC=D // 128, KCM=(QK + LV) // 128, KCF=DFF // 128)
    o = {}
    o["qkv"] = 0
    o["z"] = 3 * QK
    o["a"] = 4 * QK
    o["b"] = 4 * QK + HG
    o["gq"] = 4 * QK + 2 * HG
    o["gk"] = o["gq"] + LQK
    o["gv"] = o["gk"] + LQK
    o["r"] = o["gv"] + LV
    o["lr"] = o["r"] + LV
    c["off"] = o
    c["DIN"] = o["lr"] + 16
    return c


def build(cfg):
    D, HG, HL, DFF, NS, NF = cfg["D"], cfg["HG"], cfg["HL"], cfg["DFF"], cfg["NS"], cfg["NF"]
    QK, LQK, LV, MIX, KC, KCM, KCF, DIN = (cfg[k] for k in ("QK", "LQK", "LV", "MIX", "KC", "KCM", "KCF", "DIN"))
    off = cfg["off"]
    NT = NS + NF
    GH = min(4, HG)
    NG = HG // GH
    GL = min(2, HL)
    NGL = HL // GL
    NB = T // 128
    NCH = T // 64

    nc = bass.Bass("TRN2", target_bir_lowering=False)

    def din(name, shape, dt=F32):
        return nc.dram_tensor(name, list(shape), dt, kind="ExternalInput").ap()

    xT_d = din("xT", [D, NT * T])
    w_in_d = din("w_in", [D, DIN])
    w_out_d = din("w_out", [MIX, D])
    w_gate_d = din("w_gate", [D, DFF])
    w_up_d = din("w_up", [D, DFF])
    w_down_d = din("w_down", [DFF, D])
    nw_d = din("nw", [128, 3, KC])
    cw_d = din("cw", [128, 3 * HG, 4])
    hb_d = din("hb", [2, HG])
    gnw_d = din("gnw", [128, 3])
    w2e_d = din("w2e", [128, LQK])
    cf_d = din("cf", [128, 3, 128])
    cb_d = din("cb", [128, 6, 128], BF16)
    out_d = nc.dram_tensor("outT", [D, NF * T], F32, kind="ExternalOutput").ap()
    wi_b = nc.dram_tensor("wi_b", [D, DIN], BF16, kind="Internal").ap()
    wo_b = nc.dram_tensor("wo_b", [MIX, D], BF16, kind="Internal").ap()
    wg_b = nc.dram_tensor("wg_b", [D, DFF], BF16, kind="Internal").ap()
    wu_b = nc.dram_tensor("wu_b", [D, DFF], BF16, kind="Internal").ap()
    wd_b = nc.dram_tensor("wd_b", [DFF, D], BF16, kind="Internal").ap()

    P = Prog(nc)
    P.dbg = bool(cfg.get('dbg'))
    st = ExitStack()
    with st:
        def sb(name, shape, dt=F32):
            return st.enter_context(nc.sbuf_tensor("sb_" + name, list(shape), dt))

        xh = sb("xh", [128, KC, T])
        nT = sb("nT", [128, KC, T], BF16)
        qkv = sb("qkv", [128, 3 * HG, T], BF16)
        zs = sb("zs", [128, HG, T], BF16)
        gqk = sb("gqk", [128, 2 * HL, T], BF16)
        rs = sb("rs", [128, 2 * HL, T], BF16)
        glav = sb("glav", [128, NB, LV], BF16)
        mixT = sb("mixT", [128, KCM, T], BF16)
        actT = sb("actT", [128, KCF, T], BF16)
        wb = [sb("wb%d" % i, [128, 16, 256], BF16) for i in range(4)]
        wsm = sb("wsm", [128, KC, 32], BF16)
        nw = sb("nw", [128, 3, KC])
        cw = sb("cw", [128, 3 * HG, 4])
        gnw = sb("gnw", [128, 3])
        cf = sb("cf", [128, 3, 128])
        cb = sb("cb", [128, 6, 128], BF16)
        hbr = sb("hbr", [64, 2, HG])
        negA = sb("negA", [64, HG])
        halo = sb("halo", [128, 3 * HG, 4], BF16)
        pre = [sb("pre%d" % i, [128, T + 4], BF16) for i in range(2)]
        ybuf = [sb("ybuf%d" % i, [128, T]) for i in range(2)]
        sqb = [sb("sqb%d" % i, [128, T], BF16) for i in range(2)]
        lnb = [sb("lnb%d" % i, [128, T]) for i in range(2)]
        rstd = sb("rstd", [128, T])
        sgb = [sb("sgb%d" % i, [128, T]) for i in range(2)]
        ost = [sb("ost%d" % i, [128, T]) for i in range(2)]
        lrT = sb("lrT", [128, T], BF16); w2eb = sb("w2eb", [128, LQK], BF16)
        ab = sb("ab", [64, NCH, 2 * HG])
        g_xg = sb("g_xg", [64, NCH, HG]); g_g = sb("g_g", [64, NCH, HG]); g_en = sb("g_en", [64, NCH, HG]); g_beta = sb("g_beta", [64, NCH, HG])
        g_gc = sb("g_gc", [64, HG]); g_eg = sb("g_eg", [64, HG]); g_egl = sb("g_egl", [128, HG]); g_dgl = sb("g_dgl", [64, HG])
        g_kd = sb("g_kd", [64, HG]); g_sbe = sb("g_sbe", [64, HG])
        g_LG = sb("g_LG", [64, HG, 64], BF16); g_LGl = sb("g_LGl", [64, HG, 64], BF16); g_gh = sb("g_gh", [64, NCH, HG], BF16); g_gl2 = sb("g_gl2", [64, NCH, HG], BF16); g_dec = sb("g_dec", [64, HG, 64]); g_dmS = sb("g_dmS", [64, HG, 64]); g_dmI = sb("g_dmI", [64, HG, 64])
        g_A1 = sb("g_A1", [64, HG, 64]); g_A = sb("g_A", [64, HG, 64], BF16); g_B = sb("g_B", [64, HG, 64], BF16)
        g_Pm = sb("g_Pm", [64, HG, 64], BF16); g_PT = sb("g_PT", [64, HG, 64], BF16)
        g_Pk = [sb("g_Pk%d" % i, [64, HG, 64], BF16) for i in range(2)]
        g_Qk = [sb("g_Qk%d" % i, [64, HG, 64], BF16) for i in range(2)]
        g_Y = [sb("g_Y%d" % i, [64, HG, 64], BF16) for i in range(2)]
        g_kbe = sb("g_kbe", [64, HG, 128], BF16); g_kdec = sb("g_kdec", [64, HG, 128], BF16); g_vb = sb("g_vb", [64, HG, 128], BF16)
        g_u = sb("g_u", [64, HG, 128]); g_wT = sb("g_wT", [128, HG, 64], BF16); g_vn = sb("g_vn", [64, HG, 128], BF16)
        g_S = sb("g_S", [128, HG, 128]); g_Sbf = sb("g_Sbf", [128, HG, 128], BF16)
        g_o = sb("g_o", [64, HG, 128]); g_sq = sb("g_sq", [128, 1024]); g_on = sb("g_on", [64, HG, 128], BF16)
        g_ss = sb("g_ss", [64, HG]); g_rs = sb("g_rs", [64, HG])
        l_la = sb("l_la", [128, HL, 128]); l_eb = sb("l_eb", [128, HL, 128]); l_enb = sb("l_enb", [128, HL, 128]); l_ekl = sb("l_ekl", [128, HL, 128])
        l_bl = sb("l_bl", [128, HL]); l_lah = [sb("l_lah%d" % i, [128, HL, 128], BF16) for i in range(NB)]; l_lal = [sb("l_lal%d" % i, [128, HL, 128], BF16) for i in range(NB)]
        l_Qt = sb("l_Qt", [128, HL, 128], BF16); l_Kt = sb("l_Kt", [128, HL, 128], BF16); l_KhT = sb("l_KhT", [128, HL, 128], BF16)
        l_sc = sb("l_sc", [128, HL, 128], BF16); l_Kh = sb("l_Kh", [128, HL, 128], BF16)
        l_S = sb("l_S", [128, HL, 256]); l_Sbf = sb("l_Sbf", [128, HL, 256], BF16)
        l_on = sb("l_on", [128, HL, 256], BF16); l_ss = sb("l_ss", [128, HL]); l_rs = sb("l_rs", [128, HL])
        ps = st.enter_context(nc.psum_tensor("ps", [128, 8 * 512], F32))
        psb = ps[:].bitcast(BF16)

        def PSF(b, p0, p1, c0, c1):
            return ps[p0:p1, b * 512 + c0: b * 512 + c1]

        def PSB(b, p0, p1, c0, c1):
            return psb[p0:p1, b * 1024 + c0: b * 1024 + c1]

        U_ = cf[:, 0, :]; Ls = cf[:, 1, :]; mI = cf[:, 2, :]
        identb = cb[:, 0, :]; onesb = cb[:, 1, :]; ones128b = cb[:, 2, :]; Ub = cb[:, 3, :]; Un16b = cb[:, 4, :]; Lsb = cb[:, 5, :]

        def TT(eng, out, in0, in1, op, reads, writes):
            P.op(eng, lambda e: e.tensor_tensor(out=out, in0=in0, in1=in1, op=op), reads, writes)

        def TS(eng, out, in0, s1, s2, op0, op1, reads, writes):
            if op1 is None:
                P.op(eng, lambda e: e.tensor_scalar(out=out, in0=in0, scalar1=s1, scalar2=None, op0=op0), reads, writes)
            else:
                P.op(eng, lambda e: e.tensor_scalar(out=out, in0=in0, scalar1=s1, scalar2=s2, op0=op0, op1=op1), reads, writes)

        def STT(eng, out, in0, scalar, in1, op0, op1, reads, writes):
            P.op(eng, lambda e: e.scalar_tensor_tensor(out=out, in0=in0, scalar=scalar, in1=in1, op0=op0, op1=op1), reads, writes)

        def ACT(out, in_, func, reads, writes, scale=None, bias=None):
            kw = {}
            if scale is not None:
                kw["scale"] = scale
            if bias is not None:
                kw["bias"] = bias
            P.op("act", lambda e: e.activation(out=out, in_=in_, func=func, **kw), reads, writes)

        def CP(eng, out, in_, reads, writes):
            if eng == "act":
                ACT(out, in_, AF.Copy, reads, writes)
            else:
                P.op(eng, lambda e: e.tensor_copy(out=out, in_=in_), reads, writes)

        def MM(out, lhsT, rhs, start, stop, reads, writes):
            P.op("pe", lambda e: e.matmul(out, lhsT=lhsT, rhs=rhs, start=start, stop=stop), reads, writes)

        def TR(out, in_, ident, reads, writes):
            P.op("pe", lambda e: e.transpose(out, in_, ident), reads, writes)

        def ATL():
            return

        def MS(eng, ap, val, writes):
            P.op(eng, lambda e: e.memset(ap, val), (), writes)

        cnt = dict(fb=0, wb=0, bank=0, alt=0, sq=0, pre=0, y=0, ln=0, sg=0, ost=0)

        def rot(name, n):
            v = cnt[name] % n
            cnt[name] += 1
            return v

        def ld(dst, src, key, slot):
            P.dma(lambda e: e.dma_start(out=dst, in_=src), slot, writes=[key])

        ld(nw[:], nw_d[:, :, :], "nw", "c")
        ld(cw[:], cw_d[:, :, :], "cw", "c")
        ld(gnw[:], gnw_d[:, :], "gnw", "c")
        ld(cf[:], cf_d[:, :, :], "cf", "c")
        ld(cb[:], cb_d[:, :, :], "cb", "c")
        ld(hbr[:, 0, :], hb_d[0:1, :].partition_broadcast(64), "hbr", "c")
        ld(hbr[:, 1, :], hb_d[1:2, :].partition_broadcast(64), "hbr", "c")
        NCONST = P.dma_count["c"]
        for k in ("nw", "cw", "gnw", "cf", "cb", "hbr"):
            P.writers[k] = {"dma:c": ("dma", "c", NCONST)}

        def cast_w(dst, src, rows, key, slot):
            r = 0
            while r < rows:
                n = min(128, rows - r)
                P.dma(lambda e, r=r, n=n: e.dma_start(out=dst[r:r + n, :], in_=src[r:r + n, :]), slot, writes=[key], queue="pool")
                r += n
            P.writers[key] = {"dma:" + slot: ("dma", slot, P.dma_count[slot])}

        cast_w(wi_b, w_in_d, D, "wi_b", "k0")
        cast_w(wo_b, w_out_d, MIX, "wo_b", "k1")
        cast_w(wg_b, w_gate_d, D, "wg_b", "k2")
        cast_w(wu_b, w_up_d, D, "wu_b", "k3")
        cast_w(wd_b, w_down_d, DFF, "wd_b", "k4")

        P.dma(lambda e: e.dma_start(out=wsm[:, :, 0:2 * HG], in_=wi_b[:, off["a"]:off["a"] + 2 * HG].rearrange("(k p) c -> p k c", p=128)),
              "c2", reads=["wi_b"], writes=["wsm"])
        P.dma(lambda e: e.dma_start(out=wsm[:, :, 16:32], in_=wi_b[:, off["lr"]:off["lr"] + 16].rearrange("(k p) c -> p k c", p=128)),
              "c2", reads=["wi_b"], writes=["wsm"])
        P.writers["wsm"] = {"dma:c2": ("dma", "c2", 2)}

        ACT(negA[:], hbr[:, 0, :], AF.Exp, ["hbr"], ["negA"])
        TS("dve", negA[:], negA[:], -1.0, None, ALU.mult, None, ["negA"], ["negA"])
        MS("pool", halo[:], 0.0, ["halo"])
        MS("pool", g_S[:], 0.0, ["g_S"])
        MS("pool", g_Sbf[:], 0.0, ["g_Sbf"])
        MS("pool", l_S[:], 0.0, ["l_S"])
        MS("pool", l_Sbf[:], 0.0, ["l_Sbf"])
        MS("pool", lrT[:], 1.0, ["lrT"])
        P.dma(lambda e: e.dma_start(out=g_sq[:, 0:LQK], in_=w2e_d[:, :]), "c3", writes=["g_sq"])
        CP("dve", w2eb[:], g_sq[:, 0:LQK], ["g_sq"], ["w2eb"])

        def load_w(src, key, k0, nk, c0, ncols):
            slot = rot("wb", 4)
            def fn(e):
                ins = []
                kk = 0
                while kk < nk:
                    n = min(4, nk - kk)
                    ins.append(e.dma_start(
                        out=wb[slot][:, kk:kk + n, 0:ncols],
                        in_=src[(k0 + kk) * 128:(k0 + kk + n) * 128, c0:c0 + ncols].rearrange("(k p) c -> p k c", p=128)))
                    kk += n
                return ins
            P.dma(fn, "w%d" % slot, reads=[key], writes=[("wb", slot)], n=(nk + 3) // 4)
            return slot

        pstate = {"gen": None}

        def pump():
            g = pstate["gen"]
            if g is not None:
                if next(g, "END") == "END":
                    pstate["gen"] = None

        def rms_stats(src, skey, Dn, KCn, bankset=None):
            bank = rot("bank", 4) if bankset is None else bankset[rot("fb", len(bankset))]
            for kc in range(KCn):
                s = rot("sq", 2)
                ACT(sqb[s][:], src[:, kc, :], AF.Square, [skey], [("sqb", s)])
                MM(PSF(bank, 0, 128, 0, T), onesb, sqb[s][:], kc == 0, kc == KCn - 1, [("sqb", s), "cb"], [("ps", bank)])
            l = rot("ln", 2)
            ACT(lnb[l][:], PSF(bank, 0, 128, 0, T), AF.Ln, [("ps", bank)], [("lnb", l)], scale=1.0 / Dn, bias=EPS)
            ACT(rstd[:], lnb[l][:], AF.Exp, [("lnb", l)], ["rstd"], scale=-0.5)

        def normalize(src, skey, which, dst, dkey, KCn):
            for kc in range(KCn):
                STT("dve", dst[:, kc, :], src[:, kc, :], nw[:, which, kc:kc + 1], rstd[:], ALU.mult, ALU.mult,
                    [skey, "nw", "rstd"], [dkey])

        def fm_group(src, key, k0, nk, c0, ncols, rhs_fn, rkeys, evac, first, last, banks=None):
            slot = load_w(src, key, k0, nk, c0, ncols)
            nm = ncols // 128
            if first:
                fm_group.banks = [rot("bank", 4) for _ in range(nm)] if banks is None else banks
            bks = fm_group.banks
            for m in range(nm):
                for kk in range(nk):
                    MM(PSF(bks[m], 0, 128, 0, T), wb[slot][:, kk, m * 128:(m + 1) * 128], rhs_fn(k0 + kk),
                       first and kk == 0, last and kk == nk - 1, [("wb", slot)] + rkeys, [("ps", bks[m])])
                if last:
                    evac(m, PSF(bks[m], 0, 128, 0, T), ("ps", bks[m]))

        def nT_rhs(kc):
            return nT[:, kc, :]

        def evac_qkv(rt, full):
            def f(m, pap, pkey):
                r = rt + m
                b = rot("pre", 2)
                y = rot("y", 2)
                CP("pool", pre[b][:, 0:3], halo[:, r, 0:3], ["halo"], [("pre", b)])
                ACT(pre[b][:, 3:3 + T], pap, AF.Copy, [pkey], [("pre", b)])
                CP("pool", halo[:, r, 0:3], pre[b][:, T:T + 3], [("pre", b)], ["halo"])
                TS("dve", ybuf[y][:], pre[b][:, 0:T], cw[:, r, 0:1], None, ALU.mult, None, [("pre", b), "cw"], [("y", y)])
                for i in range(1, 4):
                    STT("dve", ybuf[y][:], pre[b][:, i:i + T], cw[:, r, i:i + 1], ybuf[y][:], ALU.mult, ALU.add,
                        [("pre", b), "cw", ("y", y)], [("y", y)])
                ACT(qkv[:, r, :], ybuf[y][:], AF.Silu, [("y", y)], [("qkv", r)])
            return f

        def l2norm_rows(rows):
            ATL()
            for r in rows:
                if True:
                    isq = r < HG
                    s = rot("sq", 2)
                    TT("pool", sqb[s][:], qkv[:, r, :], qkv[:, r, :], ALU.mult, [("qkv", r)], [("sqb", s)])
                    bank = rot("bank", 4)
                    MM(PSF(bank, 0, 128, 0, T), ones128b if isq else onesb, sqb[s][:], True, True, [("sqb", s), "cb"], [("ps", bank)])
                    l = rot("ln", 2)
                    ACT(lnb[l][:], PSF(bank, 0, 128, 0, T), AF.Ln, [("ps", bank)], [("lnb", l)], bias=(128.0 * EPS if isq else EPS))
                    ACT(lnb[l][:], lnb[l][:], AF.Exp, [("lnb", l)], [("lnb", l)], scale=-0.5)
                    TT("pool", qkv[:, r, :], qkv[:, r, :], lnb[l][:], ALU.mult, [("qkv", r), ("lnb", l)], [("qkv", r)])

        def evac_silu(dst, dkey, rt):
            def f(m, pap, pkey):
                ACT(dst[:, rt + m, :], pap, AF.Silu, [pkey], [dkey])
            return f

        def evac_copy(dst, dkey, rt, scale=None):
            def f(m, pap, pkey):
                if scale is None:
                    CP("dve", dst[:, rt + m, :], pap, [pkey], [dkey])
                else:
                    ACT(dst[:, rt + m, :], pap, AF.Copy, [pkey], [dkey], scale=scale)
            return f

        def in_proj(full, need_q):
            def grp(c0, ncols_total, evac_maker):
                done = 0
                while done < ncols_total:
                    n = min(256, ncols_total - done)
                    k0 = 0
                    while k0 < KC:
                        nk = min(16, KC - k0)
                        fm_group(wi_b, "wi_b", k0, nk, c0 + done, n, nT_rhs, ["nT"], evac_maker(done // 128),
                                 k0 == 0, k0 + nk == KC)
                        k0 += nk
                    done += n
            if need_q:
                grp(off["qkv"], QK, lambda r0: evac_qkv(r0, full))
            skip = cfg.get("skip", ())
            if "k" not in skip:
                grp(off["qkv"] + QK, QK, lambda r0: evac_qkv(HG + r0, full))
            if "v" not in skip:
                grp(off["qkv"] + 2 * QK, QK, lambda r0: evac_qkv(2 * HG + r0, full))
            if full:
                grp(off["z"], QK, lambda r0: evac_silu(zs, "zs", r0))
                grp(off["gq"], LQK, lambda r0: evac_copy(gqk, "gqk", r0, scale=128.0 ** -0.5))
            if "gk" not in skip:
                grp(off["gk"], LQK, lambda r0: evac_copy(gqk, "gqk", HL + r0))
            if full:
                grp(off["r"], LV, lambda r0: evac_silu(rs, "rs", r0))
            l2rows = (list(range(0, HG)) if full else []) + (list(range(HG, 2 * HG)) if "k" not in skip else [])
            l2norm_rows(l2rows)
            done = 0
            while done < LV and "gv" not in skip:
                n = min(256, LV - done)
                assert KC <= 16
                slot = load_w(wi_b, "wi_b", 0, KC, off["gv"] + done, n)
                for blk in range(NB):
                    bank = rot("bank", 4)
                    for kc in range(KC):
                        MM(PSF(bank, 0, 128, 0, n), nT[:, kc, blk * 128:(blk + 1) * 128], wb[slot][:, kc, 0:n],
                           kc == 0, kc == KC - 1, [("wb", slot), "nT"], [("ps", bank)])
                    CP("act" if blk % 2 else "dve", glav[:, blk, done:done + n], PSF(bank, 0, 128, 0, n), [("ps", bank)], [("glav", blk)])
                done += n
            for ci in range(NCH if "ab" not in skip else 0):
                bank = rot("bank", 4)
                for kc in range(KC):
                    MM(PSF(bank, 0, 64, 0, 2 * HG), nT[:, kc, ci * 64:(ci + 1) * 64], wsm[:, kc, 0:2 * HG],
                       kc == 0, kc == KC - 1, ["nT", "wsm"], [("ps", bank)])
                CP("dve", ab[:, ci, :], PSF(bank, 0, 64, 0, 2 * HG), [("ps", bank)], [("ab", ci)])
            bank = rot("bank", 4)
            for kc in range(KC):
                MM(PSF(bank, 0, 16, 0, T), wsm[:, kc, 16:32], nT[:, kc, :], kc == 0, kc == KC - 1, ["nT", "wsm"], [("ps", bank)])
            CP("dve", lrT[0:16, :], PSF(bank, 0, 16, 0, T), [("ps", bank)], ["lrT"])
            abk = [("ab", ci) for ci in range(NCH)]
            f2 = lambda t: t[:].rearrange("p c h -> p (c h)")
            TT("dve", g_xg[:], ab[:, :, 0:HG], bc_m(hbr[:, 1, :], NCH), ALU.add, abk + ["hbr"], ["g_xg"])
            ACT(f2(g_xg), f2(g_xg), AF.Exp, ["g_xg"], ["g_xg"])
            TS("dve", f2(g_xg), f2(g_xg), 1.0, None, ALU.add, None, ["g_xg"], ["g_xg"])
            ACT(f2(g_xg), f2(g_xg), AF.Ln, ["g_xg"], ["g_xg"])
            TT("dve", g_g[:], g_xg[:], bc_m(negA[:], NCH), ALU.mult, ["g_xg", "negA"], ["g_g"])
            CP("dve", g_en[:], ab[:, :, HG:2 * HG], abk, ["g_en"])
            ACT(f2(g_en), f2(g_en), AF.Exp, ["g_en"], ["g_en"], scale=-1.0)
            TS("dve", f2(g_en), f2(g_en), 1.0, None, ALU.add, None, ["g_en"], ["g_en"])
            P.op("dve", lambda e: e.reciprocal(out=f2(g_beta), in_=f2(g_en)), ["g_en"], ["g_beta"])
            CP("dve", g_gh[:], g_g[:], ["g_g"], ["g_gh"])
            TT("dve", g_gl2[:], g_g[:], g_gh[:], ALU.subtract, ["g_g", "g_gh"], ["g_gl2"])
            for blk in range(NB):
                gla_gate(blk)

        def bc_h(ap2, n):
            return ap2.unsqueeze(2).to_broadcast([ap2.shape[0], ap2.shape[1], n])

        def bc_m(ap2, h):
            return ap2.unsqueeze(1).to_broadcast([ap2.shape[0], h, ap2.shape[1]])

        def gdn_chunk(ci, full):
            cs = slice(ci * 64, ci * 64 + 64)
            gg = g_g[:, ci, :]; gbeta = g_beta[:, ci, :]; ggh = g_gh[:, ci, :]; ggl = g_gl2[:, ci, :]
            TT("dve", g_LG[:], bc_m(U_[0:64, 0:64], HG), bc_h(ggh, 64), ALU.mult, ["cf", "g_gh"], ["g_LG"])
            TT("dve", g_LGl[:], bc_m(U_[0:64, 0:64], HG), bc_h(ggl, 64), ALU.mult, ["cf", "g_gl2"], ["g_LGl"])
            pump()
            MM(PSF(7, 0, 64, 256, 256 + HG), Ub[0:64, 0:64], ggh, True, False, ["cb", "g_gh"], [("ps", 7)])
            MM(PSF(7, 0, 64, 256, 256 + HG), Ub[0:64, 0:64], ggl, False, True, ["cb", "g_gl2"], [("ps", 7)])
            MM(PSF(7, 0, 128, 272, 272 + HG), onesb[0:64, :], ggh, True, False, ["cb", "g_gh"], [("ps", 7)])
            MM(PSF(7, 0, 128, 272, 272 + HG), onesb[0:64, :], ggl, False, True, ["cb", "g_gl2"], [("ps", 7)])
            for h in range(HG):
                MM(PSF(6, 0, 64, h * 64, (h + 1) * 64), g_LG[:, h, :], Lsb[0:64, 0:64], True, False, ["g_LG", "cb"], [("ps", 6)])
                MM(PSF(6, 0, 64, h * 64, (h + 1) * 64), g_LGl[:, h, :], Lsb[0:64, 0:64], False, True, ["g_LGl", "cb"], [("ps", 6)])
            CP("dve", g_gc[:], PSF(7, 0, 64, 256, 256 + HG), [("ps", 7)], ["g_gc"])
            pump()
            ACT(g_eg[:], PSF(7, 0, 64, 256, 256 + HG), AF.Exp, [("ps", 7)], ["g_eg"])
            ACT(g_egl[:], PSF(7, 0, 128, 272, 272 + HG), AF.Exp, [("ps", 7)], ["g_egl"])
            TT("dve", g_dgl[:], PSF(7, 0, 64, 272, 272 + HG), g_gc[:], ALU.subtract, [("ps", 7), "g_gc"], ["g_dgl"])
            ACT(g_kd[:], g_dgl[:], AF.Exp, ["g_dgl"], ["g_kd"])
            TT("dve", g_sbe[:], gbeta, g_eg[:], ALU.mult, ["g_beta", "g_eg"], ["g_sbe"])
            pump()
            ACT(g_dec[:].rearrange("p h c -> p (h c)"), PSF(6, 0, 64, 0, HG * 64), AF.Exp, [("ps", 6)], ["g_dec"])
            pump()
            TT("pool", g_dmS[:], g_dec[:], bc_m(Ls[0:64, 0:64], HG), ALU.mult, ["g_dec", "cf"], ["g_dmS"])
            if full:
                TT("pool", g_dmI[:], g_dec[:], bc_m(mI[0:64, 0:64], HG), ALU.mult, ["g_dec", "cf"], ["g_dmI"])
            for h in range(HG):
                TR(PSB(2, 0, 64, h * 128, (h + 1) * 128), qkv[:, HG + h, cs], identb, [("qkv", HG + h), "cb"], [("ps", 2)])
            for h in range(HG):
                TR(PSB(3, 0, 64, h * 128, (h + 1) * 128), qkv[:, 2 * HG + h, cs], identb, [("qkv", 2 * HG + h), "cb"], [("ps", 3)])
            kt3 = PSB(2, 0, 64, 0, HG * 128).rearrange("p (h d) -> p h d", h=HG)
            pump()
            vt3 = PSB(3, 0, 64, 0, HG * 128).rearrange("p (h d) -> p h d", h=HG)
            TT("dve", g_kbe[:], kt3, bc_h(g_sbe[:], 128), ALU.mult, [("ps", 2), "g_sbe"], ["g_kbe"])
            TT("dve", g_kdec[:], kt3, bc_h(g_kd[:], 128), ALU.mult, [("ps", 2), "g_kd"], ["g_kdec"])
            TT("dve", g_vb[:], vt3, bc_h(gbeta, 128), ALU.mult, [("ps", 3), "g_beta"], ["g_vb"])
            pump()
            for h in range(HG):
                MM(PSF(4, 0, 64, h * 64, (h + 1) * 64), qkv[:, HG + h, cs], qkv[:, HG + h, cs], True, True, [("qkv", HG + h)], [("ps", 4)])
            if full:
                for h in range(HG):
                    MM(PSF(5, 0, 64, h * 64, (h + 1) * 64), qkv[:, h, cs], qkv[:, HG + h, cs], True, True, [("qkv", h), ("qkv", HG + h)], [("ps", 5)])
            G3 = PSF(4, 0, 64, 0, HG * 64).rearrange("p (h c) -> p h c", h=HG)
            pump()
            TT("dve", g_A1[:], G3, g_dmS[:], ALU.mult, [("ps", 4), "g_dmS"], ["g_A1"])
            pump()
            TT("pool", g_A[:], g_A1[:], bc_h(gbeta, 64), ALU.mult, ["g_A1", "g_beta"], ["g_A"])
            pump()
            if full:
                P3 = PSF(5, 0, 64, 0, HG * 64).rearrange("p (h c) -> p h c", h=HG)
                TT("dve", g_Pm[:], P3, g_dmI[:], ALU.mult, [("ps", 5), "g_dmI"], ["g_Pm"])
            for h in range(HG):
                TR(PSB(6, 0, 64, h * 64, (h + 1) * 64), g_A[:, h, :], identb[0:64, 0:64], ["g_A", "cb"], [("ps", 6)])
            if full:
                for h in range(HG):
                    TR(PSB(7, 0, 64, h * 64, (h + 1) * 64), g_Pm[:, h, :], identb[0:64, 0:64], ["g_Pm", "cb"], [("ps", 7)])
            B3 = PSB(6, 0, 64, 0, HG * 64).rearrange("p (h c) -> p h c", h=HG)
            pump()
            CP("act", g_B[:], B3, [("ps", 6)], ["g_B"])
            STT("dve", g_Y[0][:], B3, -1.0, bc_m(identb[0:64, 0:64], HG), ALU.mult, ALU.add, [("ps", 6), "cb"], [("g_Y", 0)])
            pump()
            if full:
                CP("act", g_PT[:], PSB(7, 0, 64, 0, HG * 64).rearrange("p (h c) -> p h c", h=HG), [("ps", 7)], ["g_PT"])
            Pp, Qp, Pk_key, Qk_key = g_A, g_B, "g_A", "g_B"
            yi = 0
            for lvl in range(1, 6):
                i = lvl % 2
                for h in range(HG):
                    MM(PSF(4, 0, 64, h * 64, (h + 1) * 64), Qp[:, h, :], Pp[:, h, :], True, True, [Pk_key, Qk_key], [("ps", 4)])
                if lvl < 5:
                    for h in range(HG):
                        MM(PSF(5, 0, 64, h * 64, (h + 1) * 64), Pp[:, h, :], Qp[:, h, :], True, True, [Pk_key, Qk_key], [("ps", 5)])
                CP("act", g_Pk[i][:].rearrange("p h c -> p (h c)"), PSF(4, 0, 64, 0, HG * 64), [("ps", 4)], [("g_Pk", i)])
                if lvl < 5:
                    CP("dve", g_Qk[i][:].rearrange("p h c -> p (h c)"), PSF(5, 0, 64, 0, HG * 64), [("ps", 5)], [("g_Qk", i)])
                pump()
                for h in range(HG):
                    MM(PSF(2, 0, 64, h * 64, (h + 1) * 64), g_Pk[i][:, h, :], g_Y[yi][:, h, :], True, True, [("g_Pk", i), ("g_Y", yi)], [("ps", 2)])
                TT("dve", g_Y[1 - yi][:].rearrange("p h c -> p (h c)"), PSF(2, 0, 64, 0, HG * 64), g_Y[yi][:].rearrange("p h c -> p (h c)"),
                   ALU.add, [("ps", 2), ("g_Y", yi)], [("g_Y", 1 - yi)])
                yi = 1 - yi
                pump()
                Pp, Qp, Pk_key, Qk_key = g_Pk[i], g_Qk[i], ("g_Pk", i), ("g_Qk", i)
            Y = g_Y[yi]
            Ykey = ("g_Y", yi)
            for gi in range(NG):
                for hh in range(GH):
                    h = gi * GH + hh
                    MM(PSF(2 + gi, 0, 64, hh * 128, (hh + 1) * 128), Y[:, h, :], g_vb[:, h, :], True, True, [Ykey, "g_vb"], [("ps", 2 + gi)])
            for h in range(HG):
                MM(PSF(6, 0, 128, h * 64, (h + 1) * 64), g_kbe[:, h, :], Y[:, h, :], True, True, [Ykey, "g_kbe"], [("ps", 6)])
            for gi in range(NG):
                CP("act", g_u[:, gi * GH:(gi + 1) * GH, :].rearrange("p h d -> p (h d)"), PSF(2 + gi, 0, 64, 0, GH * 128), [("ps", 2 + gi)], ["g_u"])
            CP("dve", g_wT[:].rearrange("p h c -> p (h c)"), PSF(6, 0, 128, 0, HG * 64), [("ps", 6)], ["g_wT"])
            pump()
            for gi in range(NG):
                for hh in range(GH):
                    h = gi * GH + hh
                    MM(PSF(4 + gi, 0, 64, hh * 128, (hh + 1) * 128), g_wT[:, h, :], g_Sbf[:, h, :], True, True, ["g_wT", "g_Sbf"], [("ps", 4 + gi)])
            for gi in range(NG):
                TT("dve", g_vn[:, gi * GH:(gi + 1) * GH, :].rearrange("p h d -> p (h d)"),
                   g_u[:, gi * GH:(gi + 1) * GH, :].rearrange("p h d -> p (h d)"), PSF(4 + gi, 0, 64, 0, GH * 128), ALU.subtract,
                   ["g_u", ("ps", 4 + gi)], ["g_vn"])
            pump()
            if full:
                for gi in range(NG):
                    for hh in range(GH):
                        h = gi * GH + hh
                        MM(PSF(4 + gi, 0, 64, hh * 128, (hh + 1) * 128), qkv[:, h, cs], g_Sbf[:, h, :], True, True, [("qkv", h), "g_Sbf"], [("ps", 4 + gi)])
                        MM(PSF(2 + gi, 0, 64, hh * 128, (hh + 1) * 128), g_PT[:, h, :], g_vn[:, h, :], True, True, ["g_PT", "g_vn"], [("ps", 2 + gi)])
            for gi in range(NG):
                for hh in range(GH):
                    h = gi * GH + hh
                    MM(PSF(6 + gi, 0, 128, hh * 128, (hh + 1) * 128), g_kdec[:, h, :], g_vn[:, h, :], True, True, ["g_kdec", "g_vn"], [("ps", 6 + gi)])
            if full:
                for gi in range(NG):
                    hs = slice(gi * GH, (gi + 1) * GH)
                    o1 = PSF(4 + gi, 0, 64, 0, GH * 128).rearrange("p (h d) -> p h d", h=GH)
                    o2 = PSF(2 + gi, 0, 64, 0, GH * 128).rearrange("p (h d) -> p h d", h=GH)
                    TT("dve", g_o[:, hs, :], o1, bc_h(g_eg[:, hs], 128), ALU.mult, [("ps", 4 + gi), "g_eg"], ["g_o"])
                    TT("dve", g_o[:, hs, :], g_o[:, hs, :], o2, ALU.add, [("ps", 2 + gi), "g_o"], ["g_o"])
            TT("pool", g_S[:], g_S[:], bc_h(g_egl[:], 128), ALU.mult, ["g_S", "g_egl"], ["g_S"])
            for gi in range(NG):
                hs = slice(gi * GH, (gi + 1) * GH)
                TT("dve", g_S[:, hs, :].rearrange("p h d -> p (h d)"), g_S[:, hs, :].rearrange("p h d -> p (h d)"),
                   PSF(6 + gi, 0, 128, 0, GH * 128), ALU.add, ["g_S", ("ps", 6 + gi)], ["g_S"])
            CP("act", g_Sbf[:], g_S[:], ["g_S"], ["g_Sbf"])
            pump()
            if full:
                ACT(g_sq[0:64, 0:HG * 128], g_o[:].rearrange("p h d -> p (h d)"), AF.Square, ["g_o"], ["g_sq"])
                P.op("dve", lambda e: e.tensor_reduce(out=g_ss[:], in_=g_sq[0:64, 0:HG * 128].rearrange("p (h d) -> p h d", h=HG), axis=AX.X, op=ALU.add),
                     ["g_sq"], ["g_ss"])
                ACT(g_rs[:], g_ss[:], AF.Ln, ["g_ss"], ["g_rs"], scale=1.0 / 128, bias=EPS)
                ACT(g_rs[:], g_rs[:], AF.Exp, ["g_rs"], ["g_rs"], scale=-0.5)
                TT("pool", g_on[:], g_o[:], bc_h(g_rs[:], 128), ALU.mult, ["g_o", "g_rs"], ["g_on"])
                for h in range(HG):
                    TR(PSB(4, 0, 128, h * 64, (h + 1) * 64), g_on[:, h, :], identb[0:64, 0:64], ["g_on", "cb"], [("ps", 4)])
                oT3 = PSB(4, 0, 128, 0, HG * 64).rearrange("p (h c) -> p h c", h=HG)
                STT("dve", mixT[:, 0:HG, cs], oT3, gnw[:, 0:1], zs[:, :, cs], ALU.mult, ALU.mult, [("ps", 4), "gnw", "zs"], ["mixT"])

            pump()

        def gla_gate(blk):
            bs = slice(blk * 128, blk * 128 + 128)
            MM(PSF(0, 0, 128, 0, LQK), lrT[:, bs], w2eb[:], True, True, ["lrT", "w2eb"], [("ps", 0)])
            la2 = l_la[:].rearrange("p h d -> p (h d)")
            ACT(la2, PSF(0, 0, 128, 0, LQK), AF.Exp, [("ps", 0)], ["l_la"], scale=-1.0)
            TS("dve", la2, la2, 1.0, None, ALU.add, None, ["l_la"], ["l_la"])
            ACT(la2, la2, AF.Ln, ["l_la"], ["l_la"])
            CP("dve", l_lah[blk][:], l_la[:], ["l_la"], [("l_lah", blk)])
            TT("dve", l_lal[blk][:], l_la[:], l_lah[blk][:], ALU.subtract, ["l_la", ("l_lah", blk)], [("l_lal", blk)])

        def gla_chunk(blk, full):
            bs = slice(blk * 128, blk * 128 + 128)
            if cfg.get('sub', 99) < 2.1:
                return
            for h in range(HL):
                MM(PSF(2, 0, 128, h * 128, (h + 1) * 128), l_lah[blk][:, h, :], Un16b, True, False, [("l_lah", blk), "cb"], [("ps", 2)])
                MM(PSF(2, 0, 128, h * 128, (h + 1) * 128), l_lal[blk][:, h, :], Un16b, False, True, [("l_lal", blk), "cb"], [("ps", 2)])
            pump()
            b3 = PSF(2, 0, 128, 0, HL * 128).rearrange("p (h i) -> p h i", h=HL)
            ACT(l_eb[:].rearrange("p h d -> p (h d)"), PSF(2, 0, 128, 0, HL * 128), AF.Exp, [("ps", 2)], ["l_eb"])
            if cfg.get("extra") == "one":
                return
            if cfg.get("extra") == "dvecp":
                CP("dve", l_enb[:].rearrange("p h d -> p (h d)"), PSF(2, 0, 128, 0, HL * 128), [("ps", 2)], ["l_enb"])
                return
            ACT(l_enb[:].rearrange("p h d -> p (h d)"), PSF(2, 0, 128, 0, HL * 128), AF.Exp, [("ps", 2)], ["l_enb"], scale=-1.0)
            if cfg.get('sub', 99) < 2.3:
                return
            CP("dve", l_bl[:], b3[:, :, 127], [("ps", 2)], ["l_bl"])
            if cfg.get('sub', 99) < 2.4:
                return
            for h in range(HL):
                ACT(l_ekl[:, h, :], PSF(2, 0, 128, h * 128, (h + 1) * 128), AF.Exp, [("ps", 2), "l_bl"], ["l_ekl"], scale=-1.0, bias=l_bl[:, h:h + 1])
            pump()
            if full:
                TT("pool", l_Qt[:], gqk[:, 0:HL, bs], l_eb[:], ALU.mult, ["gqk", "l_eb"], ["l_Qt"])
                TT("pool", l_Kt[:], gqk[:, HL:2 * HL, bs], l_enb[:], ALU.mult, ["gqk", "l_enb"], ["l_Kt"])
            TT("dve", l_KhT[:], gqk[:, HL:2 * HL, bs], l_ekl[:], ALU.mult, ["gqk", "l_ekl"], ["l_KhT"])
            pump()
            if full:
                for h in range(HL):
                    MM(PSF(3, 0, 128, h * 128, (h + 1) * 128), l_Kt[:, h, :], l_Qt[:, h, :], True, True, ["l_Kt", "l_Qt"], [("ps", 3)])
            for h in range(HL):
                TR(PSB(2, 0, 128, h * 128, (h + 1) * 128), l_KhT[:, h, :], identb, ["l_KhT", "cb"], [("ps", 2)])
            if full:
                TT("dve", l_sc[:], PSF(3, 0, 128, 0, HL * 128).rearrange("p (h i) -> p h i", h=HL), bc_m(U_, HL), ALU.mult,
                   [("ps", 3), "cf"], ["l_sc"])
            CP("act", l_Kh[:].rearrange("p h d -> p (h d)"), PSB(2, 0, 128, 0, HL * 128), [("ps", 2)], ["l_Kh"])
            pump()
            if cfg.get('sub', 99) < 4:
                return
            if full:
                for gi in range(NGL):
                    for hh in range(GL):
                        h = gi * GL + hh
                        MM(PSF(4 + gi, 0, 128, hh * 256, (hh + 1) * 256), l_Qt[:, h, :], l_Sbf[:, h, :], True, False, ["l_Qt", "l_Sbf"], [("ps", 4 + gi)])
                        MM(PSF(4 + gi, 0, 128, hh * 256, (hh + 1) * 256), l_sc[:, h, :], glav[:, blk, h * 256:(h + 1) * 256], False, True,
                           ["l_sc", ("glav", blk)], [("ps", 4 + gi)])
            for gi in range(NGL):
                for hh in range(GL):
                    h = gi * GL + hh
                    MM(PSF(6 + gi, 0, 128, hh * 256, (hh + 1) * 256), l_Kh[:, h, :], glav[:, blk, h * 256:(h + 1) * 256], True, True,
                       ["l_Kh", ("glav", blk)], [("ps", 6 + gi)])
            for gi in range(NGL):
                for hh in range(GL):
                    h = gi * GL + hh
                    STT("dve", l_S[:, h, :], l_S[:, h, :], l_eb[:, h, 127:128], PSF(6 + gi, 0, 128, hh * 256, (hh + 1) * 256), ALU.mult, ALU.add,
                        ["l_S", "l_eb", ("ps", 6 + gi)], ["l_S"])
            if cfg.get('sub', 99) < 5:
                return
            if full:
                for gi in range(NGL):
                    ACT(g_sq[:, gi * 512:gi * 512 + GL * 256], PSF(4 + gi, 0, 128, 0, GL * 256), AF.Square, [("ps", 4 + gi)], ["g_sq"])
                P.op("dve", lambda e: e.tensor_reduce(out=l_ss[:], in_=g_sq[:, 0:HL * 256].rearrange("p (h d) -> p h d", h=HL), axis=AX.X, op=ALU.add),
                     ["g_sq"], ["l_ss"])
                ACT(l_rs[:], l_ss[:], AF.Ln, ["l_ss"], ["l_rs"], scale=1.0 / 256, bias=EPS)
                ACT(l_rs[:], l_rs[:], AF.Exp, ["l_rs"], ["l_rs"], scale=-0.5)
                for gi in range(NGL):
                    hs = slice(gi * GL, (gi + 1) * GL)
                    TT("dve", l_on[:, hs, :], PSF(4 + gi, 0, 128, 0, GL * 256).rearrange("p (h d) -> p h d", h=GL), bc_h(l_rs[:, hs], 256), ALU.mult,
                       [("ps", 4 + gi), "l_rs"], ["l_on"])
            CP("act", l_Sbf[:], l_S[:], ["l_S"], ["l_Sbf"])
            pump()
            if full:
                for h in range(HL):
                    for half in range(2):
                        j = 2 * h + half
                        TR(PSB(3, 0, 128, j * 128, (j + 1) * 128), l_on[:, h, half * 128:(half + 1) * 128], identb, ["l_on", "cb"], [("ps", 3)])
                oT4 = PSB(3, 0, 128, 0, 2 * HL * 128).rearrange("p (h two i) -> p h two i", h=HL, two=2)
                rs4 = rs[:, :, bs].rearrange("p (h two) i -> p h two i", two=2)
                mx4 = mixT[:, HG:HG + 2 * HL, bs].rearrange("p (h two) i -> p h two i", two=2)
                for half in range(2):
                    STT("dve", mx4[:, :, half, :], oT4[:, :, half, :], gnw[:, 1 + half:2 + half], rs4[:, :, half, :], ALU.mult, ALU.mult,
                        [("ps", 3), "gnw", "rs"], ["mixT"])
            pump()

        actT_flat = actT[:].rearrange("p k t -> p (k t)")
        XA = actT_flat[:, 0:KC * T * 2].bitcast(F32).rearrange("p (k t) -> p k t", k=KC)

        def evac_res(rt0):
            def f(m, pap, pkey):
                TT("dve", xh[:, rt0 + m, :], xh[:, rt0 + m, :], pap, ALU.add, ["xh", pkey], ["xh"])
            return f

        def out_proj(ti):
            tcols = slice(ti * T, (ti + 1) * T)
            P.dma(lambda e: e.dma_start(out=xh[:], in_=xT_d[:, tcols].rearrange("(k p) t -> p k t", p=128)), "x2", writes=["xh"])
            done = 0
            while done < D:
                n = min(256, D - done)
                k0 = 0
                while k0 < KCM:
                    nk = min(16, KCM - k0)
                    fm_group(wo_b, "wo_b", k0, nk, done, n, lambda kc: mixT[:, kc, :], ["mixT"], evac_res(done // 128), k0 == 0, k0 + nk == KCM)
                    k0 += nk
                done += n

        def F_gen(ti):
            FB = (0, 1)
            rms_stats(xh, "xh", D, KC, bankset=FB)
            normalize(xh, "xh", 1, nT, "nT", KC)
            yield 1
            done = 0
            while done < DFF:
                n = min(256, DFF - done)
                nm = n // 128
                sl_g = load_w(wg_b, "wg_b", 0, KC, done, n)
                sl_u = load_w(wu_b, "wu_b", 0, KC, done, n)
                for m in range(nm):
                    for kc in range(KC):
                        MM(PSF(0, 0, 128, 0, T), wb[sl_g][:, kc, m * 128:(m + 1) * 128], nT[:, kc, :], kc == 0, kc == KC - 1,
                           [("wb", sl_g), "nT"], [("ps", 0)])
                    sg = rot("sg", 2)
                    ACT(sgb[sg][:], PSF(0, 0, 128, 0, T), AF.Silu, [("ps", 0)], [("sgb", sg)])
                    yield 1
                    for kc in range(KC):
                        MM(PSF(1, 0, 128, 0, T), wb[sl_u][:, kc, m * 128:(m + 1) * 128], nT[:, kc, :], kc == 0, kc == KC - 1,
                           [("wb", sl_u), "nT"], [("ps", 1)])
                    TT("dve", actT[:, done // 128 + m, :], sgb[sg][:], PSF(1, 0, 128, 0, T), ALU.mult, [("sgb", sg), ("ps", 1)], ["actT"])
                    yield 1
                done += n
            done = 0
            while done < D:
                n = min(256, D - done)
                nm = n // 128
                acc = [(m, 0) for m in range(nm)]
                k0 = 0
                while k0 < KCF:
                    nk = min(16, KCF - k0)
                    slot = load_w(wd_b, "wd_b", k0, nk, done, n)
                    for m in range(nm):
                        bk, co = acc[m]
                        for kk in range(nk):
                            MM(PSF(bk, 0, 128, co, co + T), wb[slot][:, kk, m * 128:(m + 1) * 128], actT[:, k0 + kk, :],
                               k0 == 0 and kk == 0, k0 + nk == KCF and kk == nk - 1, [("wb", slot), "actT"], [("ps", bk)])
                        yield 1
                    k0 += nk
                for m in range(nm):
                    bk, co = acc[m]
                    evac_res(done // 128)(m, PSF(bk, 0, 128, co, co + T), ("ps", bk))
                done += n
            rms_stats(xh, "xh", D, KC, bankset=FB)
            fcols = slice((ti - NS) * T, (ti - NS + 1) * T)
            for kc in range(KC):
                o = rot("ost", 2)
                STT("dve", ost[o][:], xh[:, kc, :], nw[:, 2, kc:kc + 1], rstd[:], ALU.mult, ALU.mult,
                    ["xh", "nw", "rstd"], [("ost", o)])
                P.dma(lambda e, o=o, kc=kc, fcols=fcols: e.dma_start(out=out_d[kc * 128:(kc + 1) * 128, fcols], in_=ost[o][:]),
                      "o%d" % o, reads=[("ost", o)], is_output=True)
                if kc % 4 == 3:
                    yield 1

        def drain():
            while pstate["gen"] is not None:
                pump()

        prev_full = None
        for ti in range(NT):
            if ti >= cfg.get("ntiles", 999):
                break
            full = ti >= NS
            tcols = slice(ti * T, (ti + 1) * T)
            P.dma(lambda e, tcols=tcols: e.dma_start(out=XA, in_=xT_d[:, tcols].rearrange("(k p) t -> p k t", p=128)),
                  "x", writes=["actT"])
            rms_stats(XA, "actT", D, KC)
            normalize(XA, "actT", 0, nT, "nT", KC)
            in_proj(full, full or ti == NS - 1)
            if prev_full is not None:
                out_proj(prev_full)
                pstate["gen"] = F_gen(prev_full)
            for blk in range(NB):
                gla_chunk(blk, full)
                gdn_chunk(2 * blk, full)
                gdn_chunk(2 * blk + 1, full)
            drain()
            prev_full = ti if full else None
        if prev_full is not None:
            out_proj(prev_full)
            pstate["gen"] = F_gen(prev_full)
            drain()
        P.emit()
    return nc


def host_consts():
    k = np.arange(128)
    U = (k[:, None] <= k[None, :]).astype(np.float32)
    Ls = (k[:, None] > k[None, :]).astype(np.float32)
    mI = (k[:, None] >= k[None, :]).astype(np.float32)
    cf = np.stack([U, Ls, mI], axis=1).astype(np.float32)
    bf = ml_dtypes.bfloat16
    cb = np.stack([np.eye(128, dtype=np.float32), np.ones((128, 128), np.float32), np.full((128, 128), 128.0, np.float32), U, -U / 16.0, Ls], axis=1).astype(bf)
    return np.ascontiguousarray(cf), np.ascontiguousarray(cb)


def prep_shared(cfg, inp):
    D, HG, HL, KC = cfg["D"], cfg["HG"], cfg["HL"], cfg["KC"]
    f = lambda a: np.ascontiguousarray(np.asarray(a, dtype=np.float32))
    def pk(v):
        return f(v).reshape(KC, 128).T
    nw = np.ascontiguousarray(np.stack([pk(inp["attn_norm_w"][0]), pk(inp["ffn_norm_w"][0]), pk(inp["final_norm_w"])], axis=1))
    cwf = f(inp["gdn_conv_w"][0])
    cw = np.ascontiguousarray(cwf.reshape(4, 3 * HG, 128).transpose(2, 1, 0))
    hb = np.ascontiguousarray(np.stack([f(inp["gdn_a_log"][0]), f(inp["gdn_dt_bias"][0])], axis=0))
    glw = f(inp["gla_norm_w"][0])
    gnw = np.ascontiguousarray(np.stack([f(inp["gdn_norm_w"][0]), glw[0:128], glw[128:256]], axis=1))
    w2e = np.ascontiguousarray(np.concatenate([f(inp["gla_gate_w2"][0]), f(inp["gla_gate_b"][0])[None, :], np.zeros((111, inp["gla_gate_b"].shape[-1]), np.float32)], axis=0))
    cf, cb = host_consts()
    return dict(w_in=f(inp["w_in"][0]), w_out=f(inp["w_out"][0]), w_gate=f(inp["w_gate"][0]), w_up=f(inp["w_up"][0]),
                w_down=f(inp["w_down"][0]), nw=nw, cw=cw, hb=hb, gnw=gnw, w2e=w2e, cf=cf, cb=cb)


def core_sequences(cfg, x, meta, S):
    D, NS, NF = cfg["D"], cfg["NS"], cfg["NF"]
    B = x.shape[0]
    half = S // 2
    assert NF * T == half and NS * T >= half + N_META
    seqs = []
    for b in range(B):
        for p in range(2):
            seq = np.zeros((NS * T + NF * T, D), np.float32)
            if p == 0:
                seq[NS * T - N_META:NS * T] = meta
                seq[NS * T:] = x[b, :half]
            else:
                seq[NS * T - half - N_META:NS * T - half] = meta
                seq[NS * T - half:NS * T] = x[b, :half]
                seq[NS * T:] = x[b, half:]
            seqs.append((b, p, np.ascontiguousarray(seq.T)))
    return seqs


def kernel(x, meta_tokens, attn_norm_w, w_in, gdn_conv_w, gdn_a_log, gdn_dt_bias, gdn_norm_w,
           gla_gate_w2, gla_gate_b, gla_norm_w, w_out, ffn_norm_w, w_gate, w_up, w_down, final_norm_w):
    x = np.asarray(x, dtype=np.float32)
    B, S, D = x.shape
    half = S // 2
    NF = half // T
    NS = -(-(half + N_META) // T)
    cfg = make_cfg(D, 8, 4, w_gate.shape[-1], NS, NF)
    inp = dict(attn_norm_w=attn_norm_w, w_in=w_in, gdn_conv_w=gdn_conv_w, gdn_a_log=gdn_a_log, gdn_dt_bias=gdn_dt_bias,
               gdn_norm_w=gdn_norm_w, gla_gate_w2=gla_gate_w2, gla_gate_b=gla_gate_b, gla_norm_w=gla_norm_w, w_out=w_out,
               ffn_norm_w=ffn_norm_w, w_gate=w_gate, w_up=w_up, w_down=w_down, final_norm_w=final_norm_w)
    inp = {k: np.asarray(v) for k, v in inp.items()}
    shared = prep_shared(cfg, inp)
    seqs = core_sequences(cfg, x, np.asarray(meta_tokens, dtype=np.float32), S)
    nc = build(cfg)
    in_maps = [dict(shared, xT=s[2]) for s in seqs]
    res = run_bass_kernel_spmd(nc, in_maps, core_ids=list(range(len(in_maps))))
    out = np.empty((B, S, D), np.float32)
    for (b, p, _), r in zip(seqs, res.results):
        out[b, p * half:(p + 1) * half, :] = r["outT"].T
    return out
```
